# Optimizing a Trainium2 kernel written in Bass

```python
import jax, jax.numpy as jnp
from jax import lax
import numpy as np

D_MODEL = 4096
BATCH = 4
SEQ = 4096
DEPTH = 1

N_META = 16
CONV_DIM = D_MODEL
CONV_WIDTH = 31
SSD_INNER = 2 * D_MODEL
SSD_HEAD_DIM = 64
SSD_HEADS = SSD_INNER // SSD_HEAD_DIM
SSD_GROUPS = 8
SSD_HEADS_PER_GROUP = SSD_HEADS // SSD_GROUPS
SSD_STATE = 128
SSD_CONV_WIDTH = 4
SSD_CONV_DIM = SSD_INNER + 2 * SSD_GROUPS * SSD_STATE
CHUNK = 128
N_BRANCH = 2
IN_SPLIT_SIZES = (CONV_DIM, CONV_DIM, CONV_DIM,
                  SSD_INNER, SSD_CONV_DIM, SSD_HEADS,
                  N_BRANCH * D_MODEL)
IN_COLS = sum(IN_SPLIT_SIZES)
LN_EPS = 1e-5
RMS_EPS = 1e-5
DEEPNORM_ALPHA = (2.0 * DEPTH) ** 0.25
DEEPNORM_BETA = (8.0 * DEPTH) ** -0.25

kernel_name = "conformer_ssd_gated_hybrid_deepnorm"


def _layer_norm(x, g, b):
    xf = x.astype(jnp.float32)
    mu = jnp.mean(xf, axis=-1, keepdims=True)
    var = jnp.mean(jnp.square(xf - mu), axis=-1, keepdims=True)
    y = (xf - mu) * lax.rsqrt(var + LN_EPS)
    return (y * g.astype(jnp.float32) + b.astype(jnp.float32)).astype(x.dtype)


def _gated_group_rmsnorm(y, z, g):
    b, l, c = y.shape
    yz = (y * jax.nn.silu(z)).astype(jnp.float32).reshape(b, l, SSD_GROUPS, c // SSD_GROUPS)
    yz = yz * lax.rsqrt(jnp.mean(jnp.square(yz), axis=-1, keepdims=True) + RMS_EPS)
    return (yz.reshape(b, l, c) * g.astype(jnp.float32)).astype(y.dtype)


def _causal_depthwise_conv(x, w, bias):
    k = w.shape[0]
    out = lax.conv_general_dilated(
        x, w[:, None, :].astype(x.dtype), window_strides=(1,), padding=[(k - 1, 0)],
        dimension_numbers=('NWC', 'WIO', 'NWC'), feature_group_count=x.shape[-1])
    return out + bias


def _segsum_exp(a_cs):
    q = a_cs.shape[-1]
    diff = a_cs[..., :, None] - a_cs[..., None, :]
    mask = jnp.tril(jnp.ones((q, q), dtype=bool))
    return jnp.exp(jnp.where(mask, diff, -jnp.inf))


def _ssd_chunked(xh, dt, a, bm, cm):
    b, l, h, p = xh.shape
    pad = CHUNK - N_META
    t = l + pad
    nc = t // CHUNK
    g, r, n = SSD_GROUPS, SSD_HEADS_PER_GROUP, SSD_STATE
    xh = xh.astype(jnp.float32)
    xdt = jnp.pad(xh * dt[..., None], ((0, 0), (pad, 0), (0, 0), (0, 0))).reshape(b, nc, CHUNK, g, r, p)
    da = jnp.pad(dt * a, ((0, 0), (pad, 0), (0, 0))).reshape(b, nc, CHUNK, g, r).transpose(0, 3, 4, 1, 2)
    bc = jnp.pad(bm.astype(jnp.float32), ((0, 0), (pad, 0), (0, 0), (0, 0))).reshape(b, nc, CHUNK, g, n)
    cc = jnp.pad(cm.astype(jnp.float32), ((0, 0), (pad, 0), (0, 0), (0, 0))).reshape(b, nc, CHUNK, g, n)
    a_cs = jnp.cumsum(da, axis=-1)
    decay_mat = _segsum_exp(a_cs)
    cb = jnp.einsum('bclgn,bcsgn->bgcls', cc, bc)
    y_diag = jnp.einsum('bgcls,bgrcls,bcsgrp->bclgrp', cb, decay_mat, xdt)
    decay_states = jnp.exp(a_cs[..., -1:] - a_cs)
    states = jnp.einsum('bclgn,bgrcl,bclgrp->cbgrpn', bc, decay_states, xdt)
    chunk_decay = jnp.moveaxis(jnp.exp(a_cs[..., -1]), 3, 0)

    def step(hstate, inp):
        s_c, d_c = inp
        return hstate * d_c[..., None, None] + s_c, hstate

    h0 = jnp.zeros((b, g, r, p, n), jnp.float32)
    _, h_prev = lax.scan(step, h0, (states, chunk_decay))
    y_off = jnp.einsum('bclgn,cbgrpn,bgrcl->bclgrp', cc, h_prev, jnp.exp(a_cs))
    y = (y_diag + y_off).reshape(b, t, h, p)
    return y[:, pad:]


def setup_inputs(seed: int = 0) -> dict:
    key = jax.random.key(seed)
    ks = jax.random.split(key, 24)
    f32 = jnp.float32
    nrm = lambda k, s, sc: jax.random.normal(k, s, f32) * sc
    dt0 = jnp.exp(jax.random.uniform(ks[12], (DEPTH, SSD_HEADS), f32, np.log(1e-3), np.log(1e-1)))
    return {
        "x": nrm(ks[0], (BATCH, SEQ, D_MODEL), 1.0),
        "meta_tokens": nrm(ks[1], (N_META, D_MODEL), 1.0),
        "ln_in_g": 1.0 + nrm(ks[2], (D_MODEL,), 0.02),
        "ln_in_b": nrm(ks[3], (D_MODEL,), 0.02),
        "w_in": nrm(ks[4], (DEPTH, D_MODEL, IN_COLS), D_MODEL ** -0.5),
        "b_gate": nrm(ks[5], (DEPTH, N_BRANCH * D_MODEL), 0.02),
        "conv_w": nrm(ks[6], (DEPTH, CONV_WIDTH, CONV_DIM), CONV_WIDTH ** -0.5),
        "conv_b": nrm(ks[7], (DEPTH, CONV_DIM), 0.02),
        "conv_ln_g": 1.0 + nrm(ks[8], (DEPTH, CONV_DIM), 0.02),
        "conv_ln_b": nrm(ks[9], (DEPTH, CONV_DIM), 0.02),
        "w_conv_out": nrm(ks[10], (DEPTH, CONV_DIM, D_MODEL), DEEPNORM_BETA * CONV_DIM ** -0.5),
        "ssd_conv_w": nrm(ks[11], (DEPTH, SSD_CONV_WIDTH, SSD_CONV_DIM), SSD_CONV_WIDTH ** -0.5),
        "ssd_conv_b": nrm(ks[13], (DEPTH, SSD_CONV_DIM), 0.02),
        "dt_bias": dt0 + jnp.log(-jnp.expm1(-dt0)),
        "a_log": jnp.log(jax.random.uniform(ks[14], (DEPTH, SSD_HEADS), f32, 1.0, 16.0)),
        "d_skip": 1.0 + nrm(ks[15], (DEPTH, SSD_HEADS), 0.02),
        "ssd_norm_g": 1.0 + nrm(ks[16], (DEPTH, SSD_INNER), 0.02),
        "w_ssd_out": nrm(ks[17], (DEPTH, SSD_INNER, D_MODEL), DEEPNORM_BETA * SSD_INNER ** -0.5),
        "w_out": nrm(ks[18], (DEPTH, D_MODEL, D_MODEL), DEEPNORM_BETA * D_MODEL ** -0.5),
        "ln_out_g": 1.0 + nrm(ks[19], (DEPTH, D_MODEL), 0.02),
        "ln_out_b": nrm(ks[20], (DEPTH, D_MODEL), 0.02),
    }


def reference(x, meta_tokens, ln_in_g, ln_in_b, w_in, b_gate, conv_w, conv_b, conv_ln_g, conv_ln_b,
              w_conv_out, ssd_conv_w, ssd_conv_b, dt_bias, a_log, d_skip, ssd_norm_g, w_ssd_out,
              w_out, ln_out_g, ln_out_b):
    b = x.shape[0]
    meta = jnp.broadcast_to(meta_tokens[None].astype(x.dtype), (b, N_META, D_MODEL))
    h = _layer_norm(jnp.concatenate([meta, x], axis=1), ln_in_g, ln_in_b)
    l = h.shape[1]
    split_at = [int(s) for s in np.cumsum(IN_SPLIT_SIZES)[:-1]]
    for i in range(DEPTH):
        proj = jnp.einsum('bld,de->ble', h, w_in[i])
        c_val, c_glu, c_gate, z, xbc, dt_raw, gates = jnp.split(proj, split_at, axis=-1)
        v = c_val * jax.nn.sigmoid(c_glu)
        v = _causal_depthwise_conv(v, conv_w[i], conv_b[i])
        v = jax.nn.silu(_layer_norm(v, conv_ln_g[i], conv_ln_b[i])) * jax.nn.silu(c_gate)
        y_conv = jnp.einsum('blc,cd->bld', v, w_conv_out[i])
        xbc = jax.nn.silu(_causal_depthwise_conv(xbc, ssd_conv_w[i], ssd_conv_b[i]))
        xs, bm, cm = jnp.split(xbc, [SSD_INNER, SSD_INNER + SSD_GROUPS * SSD_STATE], axis=-1)
        xh = xs.reshape(b, l, SSD_HEADS, SSD_HEAD_DIM)
        bm = bm.reshape(b, l, SSD_GROUPS, SSD_STATE)
        cm = cm.reshape(b, l, SSD_GROUPS, SSD_STATE)
        dt = jax.nn.softplus(dt_raw.astype(jnp.float32) + dt_bias[i].astype(jnp.float32))
        a = -jnp.exp(a_log[i].astype(jnp.float32))
        y = _ssd_chunked(xh, dt, a, bm, cm) + d_skip[i].astype(jnp.float32)[:, None] * xh.astype(jnp.float32)
        y = _gated_group_rmsnorm(y.reshape(b, l, SSD_INNER).astype(h.dtype), z, ssd_norm_g[i])
        y_ssd = jnp.einsum('blc,cd->bld', y, w_ssd_out[i])
        g_conv, g_ssd = jnp.split(jax.nn.sigmoid(gates + b_gate[i]), N_BRANCH, axis=-1)
        sub = jnp.einsum('bld,de->ble', g_conv * y_conv + g_ssd * y_ssd, w_out[i])
        h = _layer_norm(DEEPNORM_ALPHA * h + sub, ln_out_g[i], ln_out_b[i])
    return h[:, N_META:]
```

```python
import numpy as np
import concourse.bass as bass
import concourse.mybir as mybir
from concourse.bass_utils import run_bass_kernel_spmd

F32 = mybir.dt.float32
BF16 = mybir.dt.bfloat16
AF = mybir.ActivationFunctionType
ALU = mybir.AluOpType

D = 4096
KC = 32
NMETA = 16
SEQ = 4096
NH = 128
HD = 64
NG = 8
NST = 128
INNER = 8192
XBC = 10240
C_VAL, C_GLU, C_GATE = 0, 4096, 8192
C_Z = 12288
C_XBC = 20480
C_DT = 30720
C_GATES = 30848
IN_COLS = 39040
ALPHA = 2.0 ** 0.25
EPS = 1e-5
TMAIN = 2048
TPRE = 2176
STAGES = [("pre", 0, 1152), ("pre", 1152, 1024), ("main", 0, 1024), ("main", 1024, 1024)]
TMAX = 1152
SAME_ENGINE_SYNC = True


class Buf:
    __slots__ = ("name", "lw", "rd")

    def __init__(self, name):
        self.name = name
        self.lw = None
        self.rd = {}


class Op:
    __slots__ = ("eng", "fn", "deps", "marked", "semval", "dkey", "dcount", "idx")


class Sched:
    ENGS = ["pe", "act", "dve", "pool", "sp"]

    def __init__(self, nc, sems):
        self.nc = nc
        self.free_sems = list(sems)
        self.esem = {e: self.free_sems.pop() for e in ["pe", "act", "dve", "pool"]}
        self.ecnt = {e: 0 for e in self.ENGS}
        self.ops = {e: [] for e in self.ENGS}
        self.base = {e: 0 for e in self.ENGS}
        self.dsem = {}
        self.dcnt = {}
        self.waited = {e: {} for e in self.ENGS}
        self.nops = 0

    def _dma_sem(self, key):
        if key not in self.dsem:
            self.dsem[key] = self.free_sems.pop()
            self.dcnt[key] = 0
        return self.dsem[key]

    def add(self, eng, fn, reads=(), writes=(), dkey=None):
        deps = set()
        raw = set()
        for b in reads:
            if b.lw is not None:
                deps.add(b.lw)
                raw.add(b.lw)
        for b in writes:
            if b.lw is not None:
                deps.add(b.lw)
            for r in b.rd.values():
                deps.add(r)
        op = Op()
        op.eng = eng
        op.fn = fn
        op.marked = False
        op.semval = None
        op.dkey = dkey
        op.idx = self.base[eng] + len(self.ops[eng])
        if dkey is not None:
            self._dma_sem(dkey)
            self.dcnt[dkey] += 1
            op.dcount = self.dcnt[dkey]
            me = ("dma", dkey, op.dcount)
            grp = "dma:" + dkey
        else:
            op.dcount = None
            me = ("eng", eng, op.idx)
            grp = eng
        fd = []
        for d in deps:
            if d[0] == "eng" and d[1] == eng and dkey is None:
                if eng == "pe" or not SAME_ENGINE_SYNC or d not in raw:
                    continue
            fd.append(d)
        op.deps = fd
        self.ops[eng].append(op)
        for b in reads:
            b.rd[grp] = me
        for b in writes:
            b.lw = me
            b.rd = {}
        self.nops += 1
        return op

    def flush(self):
        nc = self.nc
        for e in self.ENGS:
            for op in self.ops[e]:
                nd = []
                for d in op.deps:
                    if d[0] == "eng":
                        i = d[2] - self.base[d[1]]
                        if i < 0:
                            continue
                        self.ops[d[1]][i].marked = True
                    else:
                        pass
                    nd.append(d)
                op.deps = nd
        for e in ["pe", "act", "dve", "pool"]:
            for op in reversed(self.ops[e]):
                if op.dkey is None:
                    op.marked = True
                    break
        for e in ["pe", "act", "dve", "pool"]:
            for op in self.ops[e]:
                if op.dkey is None and op.marked:
                    self.ecnt[e] += 1
                    op.semval = self.ecnt[e]
        final_e = dict(self.ecnt)
        final_d = {k: 16 * v for k, v in self.dcnt.items()}

        def emit(e, h):
            waited = self.waited[e]

            def wait(sem, val, key):
                if waited.get(key, 0) >= val:
                    return
                h.wait_ge(sem, val)
                waited[key] = val

            for op in self.ops[e]:
                for d in op.deps:
                    if d[0] == "eng":
                        src = self.ops[d[1]][d[2] - self.base[d[1]]]
                        wait(self.esem[d[1]], src.semval, d[1])
                    else:
                        wait(self.dsem[d[1]], 16 * d[2], "dma:" + d[1])
                ins = op.fn(h)
                if op.dkey is not None:
                    ins.then_inc(self.dsem[op.dkey], 16)
                elif op.marked:
                    ins.then_inc(self.esem[e], 1)
            for e2 in ["pe", "act", "dve", "pool"]:
                if final_e[e2] > 0:
                    wait(self.esem[e2], final_e[e2], e2)
            for k, v in final_d.items():
                if v > 0:
                    wait(self.dsem[k], v, "dma:" + k)

        with nc.Block() as block:
            @block.tensor
            def _(h):
                emit("pe", h)

            @block.scalar
            def _(h):
                emit("act", h)

            @block.vector
            def _(h):
                emit("dve", h)

            @block.gpsimd
            def _(h):
                emit("pool", h)

            @block.sync
            def _(h):
                emit("sp", h)

        for e in self.ENGS:
            self.base[e] += len(self.ops[e])
            self.ops[e] = []


def build_program(debug=False):
    nc = bass.Bass("TRN2", target_bir_lowering=False)

    def din(name, shape, dt=F32):
        return nc.dram_tensor(name, list(shape), dt, kind="ExternalInput").ap()

    xpre = din("xpre", [TPRE, D])
    xmain = din("xmain", [TMAIN, D])
    mrow = din("mrow", [128, TPRE])
    mtok = din("mtok", [128, TPRE // 128])
    w_in = din("w_in", [D, IN_COLS])
    w_co = din("w_conv_out", [D, D])
    w_so = din("w_ssd_out", [INNER, D])
    w_o = din("w_out", [D, D])
    lnig = din("ln_in_g_b", [128, D])
    lnib = din("ln_in_b_b", [128, D])
    lnog = din("ln_out_g_b", [128, D])
    lnob = din("ln_out_b_b", [128, D])
    consts = din("consts", [128, 512])
    cw = din("conv_w_p", [128, 32, 31])
    cb = din("conv_b_p", [128, 32])
    clg = din("conv_ln_g_p", [128, 32])
    clb = din("conv_ln_b_p", [128, 32])
    sw = din("ssd_conv_w_p", [128, 80, 4])
    sb = din("ssd_conv_b_p", [128, 80])
    bg = din("b_gate_p", [128, 64])
    gn = din("ssd_norm_g_p", [128, 64])
    dtb = din("dt_bias_b", [128, NH])
    alog = din("a_log_b", [128, NH])
    dsk = din("d_skip_b", [128, NH])
    out = nc.dram_tensor("out", [TMAIN, D], F32, kind="ExternalOutput").ap()

    def scratch(name, shape, dt):
        kind = "ExternalOutput" if (debug and name in debug) else "Internal"
        return nc.dram_tensor(name, list(shape), dt, kind=kind).ap()

    h_d = scratch("h_d", [TMAIN, D], F32)
    xbc_d = scratch("xbc_d", [TMAX // 128, 128, 80, 128], BF16)
    cvo_d = scratch("cvo_d", [D, TMAX], BF16)
    cg_d = scratch("cg_d", [D, TMAX], BF16)
    z_d = scratch("z_d", [INNER, TMAX], BF16)
    gt_d = scratch("gt_d", [2 * D, TMAX], BF16)
    yT_d = scratch("yT_d", [INNER, TMAX], BF16)
    S_d = scratch("S_d", [128, INNER], F32)
    wcb_d = scratch("wcb_d", [32, 128, 4096], BF16)
    wsb_d = scratch("wsb_d", [64, 128, 4096], BF16)
    wob_d = scratch("wob_d", [16, 128, 8192], BF16)
    dbg_dt = scratch("dbg_dt", [TMAX, NH], F32)

    w_in_v = w_in.rearrange("(kc p) c -> p kc c", p=128)
    w_co_v = w_co.rearrange("(kc p) c -> p kc c", p=128)
    w_so_v = w_so.rearrange("(kc p) c -> p kc c", p=128)
    w_o_v = w_o.rearrange("(kc p) c -> p kc c", p=128)

    import contextlib
    es = contextlib.ExitStack()
    with es:
        sems = [es.enter_context(nc.semaphore("s%d" % i)) for i in range(72)]
        S = Sched(nc, sems)

        uid = [0]

        def sb_t(st, name, shape, dt):
            uid[0] += 1
            return st.enter_context(nc.sbuf_tensor("%s_%d" % (name, uid[0]), list(shape), dt))

        def ps_t(st, name, shape, dt):
            uid[0] += 1
            return st.enter_context(nc.psum_tensor("%s_%d" % (name, uid[0]), list(shape), dt))

        cst = sb_t(es, "cst", [128, 512], F32)
        cstb = sb_t(es, "cstb", [128, 512], BF16)
        ident_f, tri_f, ones_f = cst[:, 0:128], cst[:, 128:256], cst[:, 384:512]
        ident_b, tri_b, negm_b, ones_b = (cstb[:, 0:128], cstb[:, 128:256],
                                          cstb[:, 256:384], cstb[:, 384:512])
        cw_t = sb_t(es, "cw_t", [128, 32, 31], F32)
        cb_t = sb_t(es, "cb_t", [128, 32], F32)
        clg_t = sb_t(es, "clg_t", [128, 32], F32)
        clb_t = sb_t(es, "clb_t", [128, 32], F32)
        sw_t = sb_t(es, "sw_t", [128, 80, 4], F32)
        sb_tl = sb_t(es, "sb_tl", [128, 80], F32)
        bg_t = sb_t(es, "bg_t", [128, 64], F32)
        gn_t = sb_t(es, "gn_t", [128, 64], F32)
        dtb_t = sb_t(es, "dtb_t", [128, NH], F32)
        a_t = sb_t(es, "a_t", [128, NH], F32)
        dsk_t = sb_t(es, "dsk_t", [128, NH], F32)
        mtok_t = sb_t(es, "mtok_t", [128, TPRE // 128], F32)
        eps_t = sb_t(es, "eps_t", [128, 2], F32)
        xh_t = sb_t(es, "xh_t", [128, 80, 3], F32)
        vh_t = sb_t(es, "vh_t", [128, 32, 30], F32)
        dt_sb = sb_t(es, "dt_sb", [128, TMAX // 128, NH], F32)
        hT = None
        B_const = Buf("const")
        B_hT = Buf("hT")
        B_dt = Buf("dt")
        B_xh = Buf("xh")
        B_vh = Buf("vh")
        B_Sd = Buf("S_d")
        B_scr = {n: Buf(n) for n in ["h_d", "xbc_d", "cvo_d", "cg_d", "z_d", "gt_d", "yT_d"]}

        def dma(eng, key, out_ap, in_ap, reads, writes):
            S.add(eng, lambda h, o=out_ap, i=in_ap: h.dma_start(out=o, in_=i), reads, writes, dkey=key)

        k = 0
        for t_, src in [(cst, consts), (cw_t, cw), (cb_t, cb), (clg_t, clg), (clb_t, clb),
                        (sw_t, sw), (sb_tl, sb), (bg_t, bg), (gn_t, gn), (dtb_t, dtb),
                        (a_t, alog), (dsk_t, dsk), (mtok_t, mtok)]:
            dma("sp", "setup", t_[:], src, [], [B_const])
            k += 1
        S.add("dve", lambda h: h.tensor_copy(out=cstb[:], in_=cst[:]), [B_const], [B_const])
        S.add("act", lambda h: h.activation(out=a_t[:], in_=a_t[:], func=AF.Exp), [B_const], [B_const])
        S.add("dve", lambda h: h.tensor_scalar(out=a_t[:], in0=a_t[:], scalar1=-1.0, scalar2=None,
                                               op0=ALU.mult), [B_const], [B_const])
        S.add("dve", lambda h: h.memset(xh_t[:], 0.0), [], [B_xh])
        S.add("dve", lambda h: h.memset(eps_t[:], EPS), [], [B_const])
        S.add("dve", lambda h: h.memset(vh_t[:], 0.0), [], [B_vh])
        with contextlib.ExitStack() as st0:
            s0 = sb_t(st0, "s0", [128, INNER], F32)
            b0 = Buf("s0")
            S.add("dve", lambda h: h.memset(s0[:], 0.0), [], [b0])
            dma("sp", "Sd", S_d, s0[:], [b0], [B_Sd])
            S.flush()

        def phase1(kind, tok0, T):
            src = xpre if kind == "pre" else xmain
            with contextlib.ExitStack() as st:
                g_b = sb_t(st, "g_b", [128, D], F32)
                b_b = sb_t(st, "b_b", [128, D], F32)
                xb = [sb_t(st, "xb%d" % i, [128, D], F32) for i in range(3)]
                hb = [sb_t(st, "hb%d" % i, [128, D], BF16) for i in range(2)]
                stt = [sb_t(st, "stt%d" % i, [128, 8, 6], F32) for i in range(2)]
                mv = [sb_t(st, "mv%d" % i, [128, 4], F32) for i in range(2)]
                pst = [ps_t(st, "pst%d" % i, [128, 1024], BF16) for i in range(4)]
                Bgb = Buf("gb")
                Bx = [Buf("xb0"), Buf("xb1"), Buf("xb2")]
                Bh = [Buf("hb0"), Buf("hb1")]
                Bs = [Buf("st0"), Buf("st1")]
                Bp = [Buf("pst%d" % i) for i in range(4)]
                dma("sp", "p1g", g_b[:], lnig, [], [Bgb])
                dma("sp", "p1g", b_b[:], lnib, [], [Bgb])
                def x_load(i):
                    s3 = i % 3
                    r0 = tok0 + i * 128
                    dma("sp", "p1x%d" % s3, xb[s3][:], src[r0:r0 + 128, :], [], [Bx[s3]])

                for k_ in range(min(3, T // 128)):
                    x_load(k_)
                BhTq = [Buf("hTq%d" % q) for q in range(4)]

                def do_tile(i):
                    s = i % 2
                    x_, h_, st_, mv_ = xb[i % 3], hb[s], stt[s], mv[s]
                    r0 = tok0 + i * 128
                    for c in range(8):
                        S.add("dve", lambda h, c=c, x_=x_, st_=st_: h.bn_stats(
                            out=st_[:, c, :], in_=x_[:, c * 512:(c + 1) * 512]), [Bx[i % 3]], [Bs[s]])
                    S.add("dve", lambda h, st_=st_, mv_=mv_: h.bn_aggr(out=mv_[:, 0:2], in_=st_[:]),
                          [Bs[s]], [Bs[s]])
                    S.add("act", lambda h, mv_=mv_: h.activation(
                        out=mv_[:, 2:3], in_=mv_[:, 1:2], func=AF.Sqrt, bias=eps_t[:, 0:1]), [Bs[s], B_const], [Bs[s]])
                    S.add("dve", lambda h, mv_=mv_: h.reciprocal(out=mv_[:, 2:3], in_=mv_[:, 2:3]), [Bs[s]], [Bs[s]])
                    S.add("dve", lambda h, mv_=mv_: h.scalar_tensor_tensor(
                        out=mv_[:, 3:4], in0=mv_[:, 0:1], scalar=-1.0, in1=mv_[:, 2:3],
                        op0=ALU.mult, op1=ALU.mult), [Bs[s]], [Bs[s]])
                    S.add("act", lambda h, x_=x_, mv_=mv_: h.activation(
                        out=x_[:], in_=x_[:], func=AF.Identity, bias=mv_[:, 3:4], scale=mv_[:, 2:3]),
                        [Bx[i % 3], Bs[s]], [Bx[i % 3]])
                    S.add("pool", lambda h, x_=x_: h.tensor_tensor(out=x_[:], in0=x_[:], in1=g_b[:],
                                                                   op=ALU.mult), [Bx[i % 3], Bgb], [Bx[i % 3]])
                    yield
                    S.add("dve", lambda h, x_=x_: h.tensor_tensor(out=x_[:], in0=x_[:], in1=b_b[:],
                                                                  op=ALU.add), [Bx[i % 3], Bgb], [Bx[i % 3]])
                    if kind == "main":
                        dma("sp", "p1h%d" % (i % 3), h_d[r0:r0 + 128, :], x_[:], [Bx[i % 3]], [B_scr["h_d"]])
                    S.add("act", lambda h, x_=x_, h_=h_: h.activation(out=h_[:], in_=x_[:], func=AF.Copy),
                          [Bx[i % 3]], [Bh[s]])
                    for q in range(4):
                        for kk in range(8):
                            kc = q * 8 + kk
                            S.add("pe", lambda h, q=q, kk=kk, kc=kc, h_=h_: h.transpose(
                                out=pst[q][:, kk * 128:(kk + 1) * 128], in_=h_[:, kc * 128:(kc + 1) * 128],
                                identity=ident_b), [Bh[s], B_const], [Bp[q]])
                        dst = hT[:, q * 8:(q + 1) * 8, i * 128:(i + 1) * 128]
                        srcp = pst[q][:].rearrange("p (a b) -> p a b", b=128)
                        if q % 2 == 0:
                            S.add("act", lambda h, dst=dst, srcp=srcp: h.activation(
                                out=dst, in_=srcp, func=AF.Copy), [Bp[q]], [BhTq[q]])
                        else:
                            S.add("dve", lambda h, dst=dst, srcp=srcp: h.tensor_copy(out=dst, in_=srcp),
                                  [Bp[q]], [BhTq[q]])
                    if i + 3 < T // 128:
                        x_load(i + 3)

                nt = T // 128
                g_cur = do_tile(0)
                next(g_cur)
                for i in range(nt):
                    g_nxt = None
                    if i + 1 < nt:
                        g_nxt = do_tile(i + 1)
                        next(g_nxt)
                    next(g_cur, None)
                    g_cur = g_nxt
                S.flush()

        def phase2(kind, tok0, T, last_pre):
            tts = []
            t = 0
            while t < T:
                n = min(512, T - t)
                tts.append((t, n))
                t += n
            NW = 3
            with contextlib.ExitStack() as st:
                wb = [sb_t(st, "wb%d" % i, [128, KC, 128], BF16) for i in range(NW)]
                Bw = [Buf("wb%d" % i) for i in range(NW)]
                pp = [ps_t(st, "pp%d" % i, [128, 512], F32) for i in range(6)]
                Bpp = [Buf("pp%d" % i) for i in range(6)]
                stg = sb_t(st, "stg", [128, 32 + TMAX], F32)
                acc = sb_t(st, "acc", [128, TMAX], F32)
                accs = [sb_t(st, "accs%d" % i, [128, TMAX], F32) for i in range(2)]
                vbs = [sb_t(st, "vbs%d" % i, [128, 32 + TMAX], BF16) for i in range(2)]
                dgs = [sb_t(st, "dgs%d" % i, [128, 15, 128], BF16) for i in range(2)]
                Baccs = [Buf("accs0"), Buf("accs1")]
                Bvbs = [Buf("vbs0"), Buf("vbs1")]
                Bdgs = [Buf("dgs0"), Buf("dgs1")]
                pending = []
                sig = sb_t(st, "sig", [128, TMAX], F32)
                ob = [sb_t(st, "ob%d" % i, [128, TMAX], BF16) for i in range(2)]
                mr = sb_t(st, "mr", [128, TMAX], F32)
                dtt = [sb_t(st, "dtt%d" % i, [128, NH], F32) for i in range(3)]
                Bstg, Bacc, Bsig, Bmr, Bdtt = Buf("stg"), Buf("acc"), Buf("sig"), Buf("mr"), Buf("dtt")
                Bob = [Buf("ob0"), Buf("ob1")]
                if kind == "pre":
                    dma("sp", "p2m", mr[:, 0:T], mrow[:, tok0:tok0 + T], [], [Bmr])
                blocks = []
                nx = 80 if (kind == "main" or last_pre) else 72
                for j in range(nx):
                    blocks.append(("xbc", j, C_XBC + j * 128))
                blocks.append(("dt", 0, C_DT))
                if kind == "main":
                    for j in range(64):
                        blocks.append(("z", j, C_Z + j * 128))
                if kind == "main" or last_pre:
                    for j in range(32):
                        blocks.append(("glu", j, C_GLU + j * 128))
                        blocks.append(("val", j, C_VAL + j * 128))
                if kind == "main":
                    for j in range(32):
                        blocks.append(("cg", j, C_GATE + j * 128))
                    for j in range(64):
                        blocks.append(("gt", j, C_GATES + j * 128))
                pcount = [0]
                ocount = [0]

                def next_ps():
                    i = pcount[0] % 6
                    pcount[0] += 1
                    return pp[i], Bpp[i]

                for bi, (bk, j, c0) in enumerate(blocks):
                    s = bi % NW
                    if pending and bk != "glu":
                        pending.pop(0)()
                    if bi >= NW and bi % 3 == 0 and conv_jobs:
                        o_ap, i_ap = conv_jobs.pop(0)
                        dma("pool", "wconv", o_ap, i_ap, [], [])
                    w_ = wb[s]
                    dma("pool", "w%d" % s, w_[:], w_in_v[:, :, c0:c0 + 128], [], [Bw[s]])
                    if bk == "dt":
                        for i in range(T // 128):
                            ps, bps = next_ps()
                            for kc in range(KC):
                                S.add("pe", lambda h, ps=ps, kc=kc, i=i, w_=w_: h.matmul(
                                    ps[:, 0:128], lhsT=hT[:, kc, i * 128:(i + 1) * 128], rhs=w_[:, kc, :],
                                    start=(kc == 0), stop=(kc == KC - 1)), [B_hT, Bw[s]], [bps])
                            t0_, t1_, t2_ = dtt
                            S.add("dve", lambda h, ps=ps: h.tensor_tensor(
                                out=t0_[:], in0=ps[:, 0:128], in1=dtb_t[:], op=ALU.add), [bps, B_const], [Bdtt])
                            S.add("act", lambda h: h.activation(out=t1_[:], in_=t0_[:], func=AF.Abs), [Bdtt], [Bdtt])
                            S.add("act", lambda h: h.activation(out=t1_[:], in_=t1_[:], func=AF.Exp, scale=-1.0),
                                  [Bdtt], [Bdtt])
                            S.add("act", lambda h: h.activation(out=t1_[:], in_=t1_[:], func=AF.Ln, bias=1.0),
                                  [Bdtt], [Bdtt])
                            dst = dt_sb[:, i, :]
                            S.add("dve", lambda h, dst=dst: h.scalar_tensor_tensor(
                                out=dst, in0=t0_[:], scalar=0.0, in1=t1_[:], op0=ALU.max, op1=ALU.add),
                                [Bdtt], [B_dt])
                            if kind == "pre":
                                ci = tok0 // 128 + i
                                S.add("dve", lambda h, dst=dst, ci=ci: h.tensor_scalar(
                                    out=dst, in0=dst, scalar1=mtok_t[:, ci:ci + 1], scalar2=None, op0=ALU.mult),
                                    [B_dt, B_const], [B_dt])
                        if debug and "dbg_dt" in debug:
                            for i in range(T // 128):
                                dma("sp", "dbg", dbg_dt[i * 128:(i + 1) * 128, :], dt_sb[:, i, :], [B_dt], [])
                        continue
                    halo_only = (kind == "pre" and bk in ("glu", "val"))
                    my_tts = tts[-1:] if halo_only else tts
                    for (t0, n) in my_tts:
                        ps, bps = next_ps()
                        for kc in range(KC):
                            S.add("pe", lambda h, ps=ps, kc=kc, t0=t0, n=n, w_=w_: h.matmul(
                                ps[:, 0:n], lhsT=w_[:, kc, :], rhs=hT[:, kc, t0:t0 + n],
                                start=(kc == 0), stop=(kc == KC - 1)), [B_hT, Bw[s]], [bps])
                        if bk == "xbc":
                            if kind == "pre":
                                S.add("dve", lambda h, ps=ps, t0=t0, n=n: h.tensor_tensor(
                                    out=stg[:, 3 + t0:3 + t0 + n], in0=ps[:, 0:n], in1=mr[:, t0:t0 + n],
                                    op=ALU.mult), [bps, Bmr], [Bstg])
                            else:
                                S.add("act", lambda h, ps=ps, t0=t0, n=n: h.activation(
                                    out=stg[:, 3 + t0:3 + t0 + n], in_=ps[:, 0:n], func=AF.Copy), [bps], [Bstg])
                        elif bk in ("z", "cg", "gt"):
                            o_ = ob[ocount[0] % 2]
                            bo = Bob[ocount[0] % 2]
                            if bk == "gt":
                                S.add("act", lambda h, ps=ps, t0=t0, n=n, o_=o_, j=j: h.activation(
                                    out=o_[:, t0:t0 + n], in_=ps[:, 0:n], func=AF.Sigmoid, bias=bg_t[:, j:j + 1]),
                                    [bps, B_const], [bo])
                            else:
                                S.add("act", lambda h, ps=ps, t0=t0, n=n, o_=o_: h.activation(
                                    out=o_[:, t0:t0 + n], in_=ps[:, 0:n], func=AF.Silu), [bps], [bo])
                        elif bk == "glu":
                            S.add("act", lambda h, ps=ps, t0=t0, n=n: h.activation(
                                out=sig[:, t0:t0 + n], in_=ps[:, 0:n], func=AF.Sigmoid), [bps], [Bsig])
                        elif bk == "val":
                            S.add("dve", lambda h, ps=ps, t0=t0, n=n: h.tensor_tensor(
                                out=stg[:, 30 + t0:30 + t0 + n], in0=ps[:, 0:n], in1=sig[:, t0:t0 + n],
                                op=ALU.mult), [bps, Bsig], [Bstg])
                            if kind == "pre":
                                S.add("dve", lambda h, t0=t0, n=n: h.tensor_tensor(
                                    out=stg[:, 30 + t0:30 + t0 + n], in0=stg[:, 30 + t0:30 + t0 + n],
                                    in1=mr[:, t0:t0 + n], op=ALU.mult), [Bstg, Bmr], [Bstg])
                    if bk == "xbc":
                        o_ = ob[ocount[0] % 2]
                        bo = Bob[ocount[0] % 2]
                        ocount[0] += 1
                        S.add("dve", lambda h, j=j: h.tensor_copy(out=stg[:, 0:3], in_=xh_t[:, j, :]),
                              [B_xh], [Bstg])
                        S.add("dve", lambda h, j=j: h.tensor_scalar(
                            out=acc[:, 0:T], in0=stg[:, 0:T], scalar1=sw_t[:, j, 0:1], scalar2=None,
                            op0=ALU.mult), [Bstg, B_const], [Bacc])
                        for kk in range(1, 4):
                            S.add("dve", lambda h, j=j, kk=kk: h.scalar_tensor_tensor(
                                out=acc[:, 0:T], in0=stg[:, kk:kk + T], scalar=sw_t[:, j, kk:kk + 1],
                                in1=acc[:, 0:T], op0=ALU.mult, op1=ALU.add), [Bstg, Bacc, B_const], [Bacc])
                        S.add("act", lambda h, j=j, o_=o_: h.activation(
                            out=o_[:, 0:T], in_=acc[:, 0:T], func=AF.Silu, bias=sb_tl[:, j:j + 1]),
                            [Bacc, B_const], [bo])
                        S.add("dve", lambda h, j=j: h.tensor_copy(out=xh_t[:, j, :], in_=stg[:, T:T + 3]),
                              [Bstg], [B_xh])
                        dma("sp", "p2o%d" % ((ocount[0] - 1) % 2),
                            xbc_d[0:T // 128, :, j, :].rearrange("c p t -> p c t"),
                            o_[:, 0:T].rearrange("p (c t) -> p c t", t=128),
                            [bo], [B_scr["xbc_d"]])
                    elif bk in ("z", "cg", "gt"):
                        o_ = ob[ocount[0] % 2]
                        bo = Bob[ocount[0] % 2]
                        ocount[0] += 1
                        dd = {"z": z_d, "cg": cg_d, "gt": gt_d}[bk]
                        dn = {"z": "z_d", "cg": "cg_d", "gt": "gt_d"}[bk]
                        dma("sp", "p2o%d" % ((ocount[0] - 1) % 2), dd[j * 128:(j + 1) * 128, 0:T], o_[:, 0:T],
                            [bo], [B_scr[dn]])
                    elif bk == "val":
                        if kind == "pre":
                            S.add("dve", lambda h, j=j: h.tensor_copy(out=vh_t[:, j, :], in_=stg[:, T:T + 30]),
                                  [Bstg], [B_vh])
                        else:
                            o_ = ob[ocount[0] % 2]
                            bo = Bob[ocount[0] % 2]
                            ocount[0] += 1
                            S.add("dve", lambda h, j=j: h.tensor_copy(out=stg[:, 0:30], in_=vh_t[:, j, :]),
                                  [B_vh], [Bstg])
                            NDV = 16
                            pj = j % 2
                            ac_, bac = accs[pj], Baccs[pj]
                            vb_, bvb = vbs[pj], Bvbs[pj]
                            dg_, bdg = dgs[pj], Bdgs[pj]
                            S.add("act", lambda h, vb_=vb_: h.activation(out=vb_[:, 0:30 + T], in_=stg[:, 0:30 + T],
                                                                         func=AF.Copy), [Bstg], [bvb])
                            for ii, kk in enumerate(range(NDV, 31)):
                                S.add("act", lambda h, j=j, kk=kk, ii=ii, dg_=dg_: h.activation(
                                    out=dg_[:, ii, :], in_=ident_b, func=AF.Copy, scale=cw_t[:, j, kk:kk + 1]),
                                    [B_const], [bdg])
                            S.add("dve", lambda h, j=j, ac_=ac_: h.tensor_scalar(
                                out=ac_[:, 0:T], in0=stg[:, 0:T], scalar1=cw_t[:, j, 0:1],
                                scalar2=cb_t[:, j:j + 1], op0=ALU.mult, op1=ALU.add), [Bstg, B_const], [bac])
                            for kk in range(1, NDV):
                                S.add("dve", lambda h, j=j, kk=kk, ac_=ac_: h.scalar_tensor_tensor(
                                    out=ac_[:, 0:T], in0=stg[:, kk:kk + T], scalar=cw_t[:, j, kk:kk + 1],
                                    in1=ac_[:, 0:T], op0=ALU.mult, op1=ALU.add), [Bstg, bac, B_const], [bac])
                            S.add("dve", lambda h, j=j: h.tensor_copy(out=vh_t[:, j, :], in_=stg[:, T:T + 30]),
                                  [Bstg], [B_vh])
                            oslot = (ocount[0] - 1) % 2

                            def pend(j=j, ac_=ac_, bac=bac, vb_=vb_, bvb=bvb, dg_=dg_, bdg=bdg, o_=o_, bo=bo, oslot=oslot):
                                for (t0, n) in tts:
                                    ps, bps = next_ps()
                                    for ii, kk in enumerate(range(NDV, 31)):
                                        S.add("pe", lambda h, ps=ps, ii=ii, kk=kk, t0=t0, n=n: h.matmul(
                                            ps[:, 0:n], lhsT=dg_[:, ii, :], rhs=vb_[:, kk + t0:kk + t0 + n],
                                            start=(ii == 0), stop=(kk == 30)), [bdg, bvb], [bps])
                                    S.add("dve", lambda h, ps=ps, t0=t0, n=n: h.tensor_tensor(
                                        out=o_[:, t0:t0 + n], in0=ac_[:, t0:t0 + n], in1=ps[:, 0:n], op=ALU.add),
                                        [bac, bps], [bo])
                                dma("sp", "p2o%d" % oslot, cvo_d[j * 128:(j + 1) * 128, 0:T],
                                    o_[:, 0:T], [bo], [B_scr["cvo_d"]])

                            pending.append(pend)
                while pending:
                    pending.pop(0)()
                S.flush()

        def phase3(kind, tok0, T):
            main = kind == "main"
            nblk = 80 if main else 72
            yT_v = yT_d.rearrange("(b p) t -> p b t", p=128)
            with contextlib.ExitStack() as st:
                Sf = sb_t(st, "Sf", [128, INNER], F32)
                Sb = sb_t(st, "Sb", [128, INNER] if main else [128, 2], BF16)
                xc = [sb_t(st, "xc%d" % i, [128, 80, 128], BF16) for i in range(2)]
                nb = 1 if main else 2
                xtoks = [sb_t(st, "xtok%d" % k_, [128, INNER + 1024], BF16) for k_ in range(nb)]
                xws = [sb_t(st, "xw%d" % k_, [128, INNER], BF16) for k_ in range(nb)]
                sms = [sb_t(st, "sm%d" % k_, [128, 8, 128], F32) for k_ in range(nb)]
                sbf = sb_t(st, "sbf", [128, 3, 128], BF16)
                cbm = sb_t(st, "cbm", [128, 8, 128], F32)
                Eh = [sb_t(st, "Eh%d" % i, [128, 512], F32) for i in range(2)]
                Mh = [sb_t(st, "Mh%d" % i, [128, 128], BF16) for i in range(8)]
                t1 = [sb_t(st, "t1_%d" % i, [128, 512], F32) for i in range(2)]
                t3 = [sb_t(st, "t3_%d" % i, [128, 512], F32) for i in range(2)]
                ytok = sb_t(st, "ytok", [128, INNER] if main else [128, 2], BF16)
                yTs = sb_t(st, "yTs", [128, 64, 128] if main else [128, 1, 2], BF16)
                psT = [ps_t(st, "psT%d" % i, [128, 1024], BF16) for i in range(2)]
                psA = [ps_t(st, "psA%d" % i, [128, 512], F32) for i in range(6)]
                BSf = [Buf("Sf%d" % i) for i in range(16)]
                BSb = [Buf("Sb%d" % i) for i in range(16)]
                Bxc = [Buf("xc0"), Buf("xc1")]
                Bsbf, Bcbm = Buf("sbf"), Buf("cbm")
                Bxtoks = [Buf("xtok%d" % k_) for k_ in range(nb)]
                Bxws = [Buf("xw%d" % k_) for k_ in range(nb)]
                Bsms = [Buf("sm%d" % k_) for k_ in range(nb)]
                BEh = [Buf("Eh%d" % i) for i in range(8)]
                BMh = [Buf("Mh%d" % i) for i in range(8)]
                Bt1 = [Buf("t10"), Buf("t11")]
                Bt3 = [Buf("t30"), Buf("t31")]
                Bytok, ByTs = Buf("ytok"), Buf("yTs")
                BpsT = [Buf("psT0"), Buf("psT1")]
                BpsA = [Buf("psA%d" % i) for i in range(6)]
                hi, lo, lo2 = [sbf[:, i, :] for i in range(3)]
                pac = [0]
                ptc = [0]

                def nA():
                    i = pac[0] % 6
                    pac[0] += 1
                    return psA[i], BpsA[i]

                def nT():
                    i = ptc[0] % 2
                    ptc[0] += 1
                    return psT[i], BpsT[i]

                dma("sp", "Sld", Sf[:], S_d, [B_Sd], BSf)
                for sl in range(16 if main else 0):
                    S.add("act", lambda h, sl=sl: h.activation(
                        out=Sb[:, sl * 512:(sl + 1) * 512], in_=Sf[:, sl * 512:(sl + 1) * 512], func=AF.Copy),
                        [BSf[sl]], [BSb[sl]])

                def bc8(ap2d):
                    return ap2d.unsqueeze(2).to_broadcast([128, 8, 64])

                nch = T // 128

                def xc_load(i):
                    cs = i % 2
                    dma("sp", "xc%d" % cs, xc[cs][:, 0:nblk, :], xbc_d[i, :, 0:nblk, :],
                        [B_scr["xbc_d"]], [Bxc[cs]])

                xc_load(0)

                def do_chunk(i, xtok, xw, sm, Bxtok, Bxw, Bsm):
                    da, acs, nacs, wst, El, cd, tmp, r_ = [sm[:, k_, :] for k_ in range(8)]
                    cs = i % 2
                    xc_ = xc[cs]
                    if i + 1 < nch:
                        xc_load(i + 1)
                    for q in range(9):
                        pT, bT = nT()
                        for kk in range(8):
                            S.add("pe", lambda h, pT=pT, kk=kk, q=q, xc_=xc_: h.transpose(
                                out=pT[:, kk * 128:(kk + 1) * 128], in_=xc_[:, q * 8 + kk, :], identity=ident_b),
                                [Bxc[cs], B_const], [bT])
                        if True:
                            S.add("act", lambda h, pT=pT, q=q: h.activation(
                                out=xtok[:, q * 1024:(q + 1) * 1024], in_=pT[:], func=AF.Copy), [bT], [Bxtok])
                        else:
                            S.add("dve", lambda h, pT=pT, q=q: h.tensor_copy(
                                out=xtok[:, q * 1024:(q + 1) * 1024], in_=pT[:]), [bT], [Bxtok])
                    dti = dt_sb[:, i, :]
                    S.add("dve", lambda h, dti=dti: h.tensor_tensor(out=da, in0=dti, in1=a_t[:], op=ALU.mult),
                          [B_dt, B_const], [Bsm])
                    pS, bS = nA()
                    S.add("pe", lambda h, pS=pS: h.matmul(pS[:, 0:128], lhsT=tri_f, rhs=da, start=True, stop=True),
                          [Bsm, B_const], [bS])
                    S.add("pe", lambda h, pS=pS: h.matmul(pS[:, 128:256], lhsT=ones_f, rhs=da, start=True, stop=True),
                          [Bsm, B_const], [bS])
                    if main:
                        S.add("pe", lambda h, pS=pS: h.matmul(pS[:, 256:384], lhsT=da, rhs=tri_f, start=True,
                                                               stop=True), [Bsm, B_const], [bS])
                    S.add("act", lambda h, pS=pS: h.activation(out=cd, in_=pS[:, 128:256], func=AF.Exp), [bS], [Bsm])
                    S.add("dve", lambda h, pS=pS: h.tensor_copy(out=acs, in_=pS[:, 0:128]), [bS], [Bsm])
                    S.add("dve", lambda h, pS=pS: h.tensor_tensor(out=tmp, in0=pS[:, 128:256], in1=acs,
                                                                  op=ALU.subtract), [bS, Bsm], [Bsm])
                    S.add("act", lambda h: h.activation(out=wst, in_=tmp, func=AF.Exp), [Bsm], [Bsm])
                    S.add("dve", lambda h, dti=dti: h.tensor_tensor(out=wst, in0=wst, in1=dti, op=ALU.mult),
                          [Bsm, B_dt], [Bsm])
                    if main:
                        S.add("act", lambda h: h.activation(out=El, in_=acs, func=AF.Exp), [Bsm], [Bsm])
                        S.add("act", lambda h, dti=dti: h.activation(out=nacs, in_=dti, func=AF.Ln), [B_dt], [Bsm])
                        S.add("dve", lambda h: h.tensor_tensor(out=nacs, in0=nacs, in1=acs, op=ALU.subtract),
                              [Bsm], [Bsm])
                        S.add("act", lambda h, pS=pS: h.activation(out=hi, in_=pS[:, 256:384], func=AF.Copy),
                              [bS], [Bsbf])
                        S.add("dve", lambda h, pS=pS: h.tensor_tensor(out=r_, in0=pS[:, 256:384], in1=hi,
                                                                      op=ALU.subtract), [bS, Bsbf], [Bsm])
                        S.add("act", lambda h: h.activation(out=lo, in_=r_, func=AF.Copy), [Bsm], [Bsbf])
                    for hh in range(2):
                        S.add("pool" if main else "dve", lambda h, hh=hh: h.tensor_tensor(
                            out=xw[:, hh * 4096:(hh + 1) * 4096].rearrange("p (a b) -> p a b", b=64),
                            in0=xtok[:, hh * 4096:(hh + 1) * 4096].rearrange("p (a b) -> p a b", b=64),
                            in1=wst[:, hh * 64:(hh + 1) * 64].unsqueeze(2).to_broadcast([128, 64, 64]),
                            op=ALU.mult), [Bxtok, Bsm], [Bxw])
                    yield
                    if main:
                        for half in range(2):
                            pC, bC = nA()
                            for gg in range(4):
                                g = half * 4 + gg
                                S.add("pe", lambda h, pC=pC, gg=gg, g=g, xc_=xc_: h.matmul(
                                    pC[:, gg * 128:(gg + 1) * 128], lhsT=xc_[:, 64 + g, :], rhs=xc_[:, 72 + g, :],
                                    start=True, stop=True), [Bxc[cs]], [bC])
                            S.add("dve", lambda h, pC=pC, half=half: h.tensor_tensor(
                                out=cbm[:, half * 4:(half + 1) * 4, :],
                                in0=pC[:].rearrange("p (a b) -> p a b", b=128),
                                in1=tri_f.unsqueeze(1).to_broadcast([128, 4, 128]), op=ALU.mult),
                                [bC, B_const], [Bcbm])
                        pYs = {}

                        def rec_D(gq, xc_=xc_, dti=dti):
                            pD, bD = nA()
                            sl2 = gq % 2
                            e_ = Eh[sl2]
                            for hh in range(4):
                                hd = gq * 4 + hh
                                sel = ident_b[:, hd:hd + 1].to_broadcast([128, 128])
                                dsl = pD[:, hh * 128:(hh + 1) * 128]
                                S.add("pe", lambda h, dsl=dsl, sel=sel: h.matmul(dsl, lhsT=sel, rhs=hi, start=True,
                                                                                  stop=False), [Bsbf, B_const], [bD])
                                S.add("pe", lambda h, dsl=dsl, sel=sel: h.matmul(dsl, lhsT=sel, rhs=lo, start=False,
                                                                                  stop=False), [Bsbf, B_const], [bD])
                                S.add("pe", lambda h, dsl=dsl: h.matmul(dsl, lhsT=ident_b, rhs=negm_b, start=False,
                                                                        stop=True), [B_const], [bD])
                            for hh in range(4):
                                hd = gq * 4 + hh
                                S.add("act", lambda h, pD=pD, hh=hh, hd=hd, e_=e_: h.activation(
                                    out=e_[:, hh * 128:(hh + 1) * 128], in_=pD[:, hh * 128:(hh + 1) * 128],
                                    func=AF.Exp, bias=nacs[:, hd:hd + 1]), [bD, Bsm], [BEh[sl2 * 4 + hh]])
                            for hh in range(4):
                                hd = gq * 4 + hh
                                g = hd // 16
                                m_ = Mh[sl2 * 4 + hh]
                                S.add("dve" if hh % 2 == 0 else "pool",
                                      lambda h, e_=e_, hh=hh, g=g, m_=m_: h.tensor_tensor(
                                          out=m_[:], in0=e_[:, hh * 128:(hh + 1) * 128], in1=cbm[:, g, :], op=ALU.mult),
                                      [BEh[sl2 * 4 + hh], Bcbm], [BMh[sl2 * 4 + hh]])

                        def rec_Y(gq, xc_=xc_, cs=cs):
                            h8 = gq // 2
                            hq = gq % 2
                            sl2 = gq % 2
                            if hq == 0:
                                pYs[h8] = nA()
                            pY, bY = pYs[h8]
                            for hh in range(4):
                                hd = gq * 4 + hh
                                m_ = Mh[sl2 * 4 + hh]
                                yo = (hq * 4 + hh) * 64
                                S.add("pe", lambda h, pY=pY, yo=yo, m_=m_, hd=hd: h.matmul(
                                    pY[:, yo:yo + 64], lhsT=m_[:], rhs=xtok[:, hd * 64:(hd + 1) * 64],
                                    start=True, stop=True), [BMh[sl2 * 4 + hh], Bxtok], [bY])
                            if hq == 0:
                                return
                            g = h8 // 2
                            pO, bO = nA()
                            S.add("pe", lambda h, pO=pO, g=g, h8=h8: h.matmul(
                                pO[:], lhsT=xc_[:, 72 + g, :], rhs=Sb[:, h8 * 512:(h8 + 1) * 512],
                                start=True, stop=True), [Bxc[cs], BSb[h8]], [bO])
                            ta, tb = t1[h8 % 2], t3[h8 % 2]
                            Bta, Btb = Bt1[h8 % 2], Bt3[h8 % 2]
                            S.add("dve", lambda h, pO=pO, ta=ta, h8=h8: h.tensor_tensor(
                                out=ta[:].rearrange("p (a b) -> p a b", b=64),
                                in0=pO[:].rearrange("p (a b) -> p a b", b=64),
                                in1=bc8(El[:, h8 * 8:(h8 + 1) * 8]), op=ALU.mult), [bO, Bsm], [Bta])
                            S.add("dve", lambda h, pY=pY, ta=ta: h.tensor_tensor(
                                out=ta[:], in0=ta[:], in1=pY[:], op=ALU.add), [bY, Bta], [Bta])
                            S.add("pool", lambda h, tb=tb, h8=h8: h.tensor_tensor(
                                out=tb[:].rearrange("p (a b) -> p a b", b=64),
                                in0=xtok[:, h8 * 512:(h8 + 1) * 512].rearrange("p (a b) -> p a b", b=64),
                                in1=bc8(dsk_t[:, h8 * 8:(h8 + 1) * 8]), op=ALU.mult), [Bxtok, B_const], [Btb])
                            S.add("pool", lambda h, ta=ta, tb=tb, h8=h8: h.tensor_tensor(
                                out=ytok[:, h8 * 512:(h8 + 1) * 512], in0=ta[:], in1=tb[:], op=ALU.add),
                                [Bta, Btb], [Bytok])

                        rec_D(0)
                        for gq in range(1, 32):
                            rec_D(gq)
                            rec_Y(gq - 1)
                        rec_Y(31)
                        for q in range(8):
                            pT, bT = nT()
                            for kk in range(8):
                                S.add("pe", lambda h, pT=pT, kk=kk, q=q: h.transpose(
                                    out=pT[:, kk * 128:(kk + 1) * 128],
                                    in_=ytok[:, (q * 8 + kk) * 128:(q * 8 + kk + 1) * 128], identity=ident_b),
                                    [Bytok, B_const], [bT])
                            dst = yTs[:, q * 8:(q + 1) * 8, :]
                            srcp = pT[:].rearrange("p (a b) -> p a b", b=128)
                            if True:
                                S.add("act", lambda h, dst=dst, srcp=srcp: h.activation(out=dst, in_=srcp, func=AF.Copy),
                                      [bT], [ByTs])
                            else:
                                S.add("dve", lambda h, dst=dst, srcp=srcp: h.tensor_copy(out=dst, in_=srcp),
                                      [bT], [ByTs])
                        dma("sp", "yTo", yT_v[:, :, i * 128:(i + 1) * 128], yTs[:], [ByTs], [B_scr["yT_d"]])
                    for sl in range(16):
                        g = sl // 2
                        pQ, bQ = nA()
                        S.add("pe", lambda h, pQ=pQ, g=g, sl=sl: h.matmul(
                            pQ[:], lhsT=xtok[:, INNER + g * 128:INNER + (g + 1) * 128],
                            rhs=xw[:, sl * 512:(sl + 1) * 512], start=True, stop=True), [Bxtok, Bxw], [bQ])
                        ssl = Sf[:, sl * 512:(sl + 1) * 512]
                        S.add("pool", lambda h, ssl=ssl, sl=sl: h.tensor_tensor(
                            out=ssl.rearrange("p (a b) -> p a b", b=64), in0=ssl.rearrange("p (a b) -> p a b", b=64),
                            in1=bc8(cd[:, sl * 8:(sl + 1) * 8]), op=ALU.mult), [BSf[sl], Bsm], [BSf[sl]])
                        S.add("dve", lambda h, ssl=ssl, pQ=pQ: h.tensor_tensor(out=ssl, in0=ssl, in1=pQ[:], op=ALU.add),
                              [BSf[sl], bQ], [BSf[sl]])
                        if main and i < nch - 1:
                            S.add("act", lambda h, ssl=ssl, sl=sl: h.activation(
                                out=Sb[:, sl * 512:(sl + 1) * 512], in_=ssl, func=AF.Copy), [BSf[sl]], [BSb[sl]])
                def start(i):
                    k_ = i % nb
                    g_ = do_chunk(i, xtoks[k_], xws[k_], sms[k_], Bxtoks[k_], Bxws[k_], Bsms[k_])
                    next(g_)
                    return g_

                if main:
                    for i in range(nch):
                        g_ = start(i)
                        next(g_, None)
                else:
                    g_cur = start(0)
                    for i in range(nch):
                        g_nxt = start(i + 1) if i + 1 < nch else None
                        next(g_cur, None)
                        g_cur = g_nxt
                dma("sp", "Sst", S_d, Sf[:], BSf, [B_Sd])
                S.flush()

        def phase4(tok0, T):
            cvo_v = cvo_d.rearrange("(b p) t -> p b t", p=128)
            cg_v = cg_d.rearrange("(b p) t -> p b t", p=128)
            yT_v = yT_d.rearrange("(b p) t -> p b t", p=128)
            for tt in range(T // 512):
                t0 = tt * 512
                stM = contextlib.ExitStack()
                mrg = sb_t(stM, "mrg", [128, 32, 512], BF16)
                Bmrg = Buf("mrg")
                with contextlib.ExitStack() as stA:
                    vv = sb_t(stA, "vv", [128, 32, 512], BF16)
                    yz = sb_t(stA, "yz", [128, 64, 512], BF16)
                    Bvv = Buf("vv")
                    Byzg = [Buf("yz%d" % k_) for k_ in range(8)]
                    with contextlib.ExitStack() as st:
                        cv = sb_t(st, "cv", [128, 32, 512], BF16)
                        cgs = [sb_t(st, "cgs%d" % i, [128, 512], BF16) for i in range(4)]
                        Bcgs = [Buf("cgs%d" % i) for i in range(4)]
                        sq = [sb_t(st, "sq%d" % i, [128, 512], BF16) for i in range(2)]
                        rs = sb_t(st, "rs", [128, 4, 512], F32)
                        u = [sb_t(st, "u%d" % i, [128, 512], F32) for i in range(2)]
                        p1 = ps_t(st, "p4s1", [128, 512], F32)
                        p2 = ps_t(st, "p4s2", [128, 512], F32)
                        Bcv, Brs = Buf("cv"), Buf("rs")
                        Bsq = [Buf("sq0"), Buf("sq1")]
                        Bu = [Buf("u0"), Buf("u1")]
                        Bp1, Bp2 = Buf("p1"), Buf("p2")
                        Bcvq = [Buf("cv%d" % k_) for k_ in range(4)]
                        for k_ in range(4):
                            dma("sp", "p4cv%d" % k_, cv[:, k_ * 8:(k_ + 1) * 8, :], cvo_v[:, k_ * 8:(k_ + 1) * 8, t0:t0 + 512],
                                [B_scr["cvo_d"]], [Bcvq[k_]])
                        def cg_load(b):
                            dma("sp", "p4cg%d" % (b % 4), cgs[b % 4][:], cg_d[b * 128:(b + 1) * 128, t0:t0 + 512],
                                [B_scr["cg_d"]], [Bcgs[b % 4]])

                        for b in range(4):
                            cg_load(b)
                        dma("sp", "p4y", yz[:], yT_v[:, :, t0:t0 + 512], [B_scr["yT_d"]], Byzg)
                        for b in range(32):
                            S.add("act", lambda h, b=b: h.activation(out=sq[b % 2][:], in_=cv[:, b, :], func=AF.Square),
                                  [Bcvq[b // 8]], [Bsq[b % 2]])
                            S.add("pe", lambda h, b=b: h.matmul(p1[:], lhsT=ones_b, rhs=cv[:, b, :], start=(b == 0),
                                                                stop=(b == 31)), [Bcvq[b // 8], B_const], [Bp1])
                            S.add("pe", lambda h, b=b: h.matmul(p2[:], lhsT=ones_b, rhs=sq[b % 2][:], start=(b == 0),
                                                                stop=(b == 31)), [Bsq[b % 2], B_const], [Bp2])
                        mean, var, rstd, nmr = [rs[:, i, :] for i in range(4)]
                        S.add("dve", lambda h: h.tensor_scalar(out=mean, in0=p1[:], scalar1=1.0 / D, scalar2=None,
                                                               op0=ALU.mult), [Bp1], [Brs])
                        S.add("dve", lambda h: h.tensor_tensor(out=var, in0=mean, in1=mean, op=ALU.mult), [Brs], [Brs])
                        S.add("dve", lambda h: h.scalar_tensor_tensor(out=var, in0=p2[:], scalar=1.0 / D, in1=var,
                                                                      op0=ALU.mult, op1=ALU.subtract), [Bp2, Brs], [Brs])
                        S.add("act", lambda h: h.activation(out=rstd, in_=var, func=AF.Sqrt, bias=eps_t[:, 0:1]),
                              [Brs, B_const], [Brs])
                        S.add("dve", lambda h: h.reciprocal(out=rstd, in_=rstd), [Brs], [Brs])
                        S.add("dve", lambda h: h.scalar_tensor_tensor(out=nmr, in0=mean, scalar=-1.0, in1=rstd,
                                                                      op0=ALU.mult, op1=ALU.mult), [Brs], [Brs])
                        for b in range(32):
                            u_ = u[b % 2]
                            S.add("dve", lambda h, b=b, u_=u_: h.tensor_tensor(out=u_[:], in0=cv[:, b, :], in1=rstd,
                                                                               op=ALU.mult), [Bcvq[b // 8], Brs], [Bu[b % 2]])
                            S.add("dve", lambda h, u_=u_: h.tensor_tensor(out=u_[:], in0=u_[:], in1=nmr, op=ALU.add),
                                  [Bu[b % 2], Brs], [Bu[b % 2]])
                            S.add("act", lambda h, b=b, u_=u_: h.activation(
                                out=u_[:], in_=u_[:], func=AF.Silu, bias=clb_t[:, b:b + 1], scale=clg_t[:, b:b + 1]),
                                [Bu[b % 2], B_const], [Bu[b % 2]])
                            S.add("dve", lambda h, b=b, u_=u_: h.tensor_tensor(out=vv[:, b, :], in0=u_[:], in1=cgs[b % 4][:],
                                                                                op=ALU.mult), [Bu[b % 2], Bcgs[b % 4]], [Bvv])
                            if b + 4 < 32:
                                cg_load(b + 4)
                        S.flush()
                    with contextlib.ExitStack() as st:
                        zt = [sb_t(st, "zt%d" % i, [128, 8, 512], BF16) for i in range(2)]
                        sq = [sb_t(st, "sq%d" % i, [128, 512], BF16) for i in range(2)]
                        rg = sb_t(st, "rg", [128, 8, 512], F32)
                        pg = [ps_t(st, "p4g%d" % i, [128, 512], F32) for i in range(2)]
                        Bzt = [Buf("zt0"), Buf("zt1")]
                        Bsq = [Buf("sq0"), Buf("sq1")]
                        Brg = Buf("rg")
                        Bpg = [Buf("pg0"), Buf("pg1")]
                        z_v = z_d.rearrange("(b p) t -> p b t", p=128)
                        Brgg = [Buf("rg%d" % k_) for k_ in range(8)]

                        def zmul(g):
                            zs = g % 2
                            dma("sp", "p4z%d" % zs, zt[zs][:], z_v[:, g * 8:(g + 1) * 8, t0:t0 + 512],
                                [B_scr["z_d"]], [Bzt[zs]])
                            S.add("dve", lambda h, g=g, zs=zs: h.tensor_tensor(
                                out=yz[:, g * 8:(g + 1) * 8, :], in0=yz[:, g * 8:(g + 1) * 8, :], in1=zt[zs][:],
                                op=ALU.mult), [Byzg[g], Bzt[zs]], [Byzg[g]])

                        zmul(0)
                        for g in range(8):
                            if g + 1 < 8:
                                zmul(g + 1)
                            for bb in range(8):
                                b = g * 8 + bb
                                S.add("act", lambda h, b=b: h.activation(out=sq[b % 2][:], in_=yz[:, b, :],
                                                                         func=AF.Square), [Byzg[g]], [Bsq[b % 2]])
                                S.add("pe", lambda h, b=b, bb=bb, g=g: h.matmul(
                                    pg[g % 2][:], lhsT=ones_b, rhs=sq[b % 2][:], start=(bb == 0), stop=(bb == 7)),
                                    [Bsq[b % 2], B_const], [Bpg[g % 2]])
                            S.add("act", lambda h, g=g: h.activation(
                                out=rg[:, g, :], in_=pg[g % 2][:], func=AF.Sqrt, bias=eps_t[:, 0:1], scale=1.0 / 1024),
                                [Bpg[g % 2], B_const], [Brgg[g]])
                            S.add("dve", lambda h, g=g: h.reciprocal(out=rg[:, g, :], in_=rg[:, g, :]), [Brgg[g]], [Brgg[g]])
                            for bb in range(8):
                                b = g * 8 + bb
                                S.add("dve", lambda h, b=b, g=g: h.scalar_tensor_tensor(
                                    out=yz[:, b, :], in0=yz[:, b, :], scalar=gn_t[:, b:b + 1], in1=rg[:, g, :],
                                    op0=ALU.mult, op1=ALU.mult), [Byzg[g], Brgg[g], B_const], [Byzg[g]])
                        S.flush()
                    with contextlib.ExitStack() as st:
                        wq = [sb_t(st, "wq%d" % i, [128, 32, 128], BF16) for i in range(4)]
                        Bwq = [Buf("wq%d" % i) for i in range(4)]
                        gts = [sb_t(st, "gts%d" % i, [128, 2, 512], BF16) for i in range(2)]
                        tm = [sb_t(st, "tm%d" % i, [128, 512], F32) for i in range(2)]
                        pa = [ps_t(st, "p4a%d" % i, [128, 512], F32) for i in range(2)]
                        pb = [ps_t(st, "p4b%d" % i, [128, 512], F32) for i in range(2)]
                        Bgts = [Buf("g0"), Buf("g1")]
                        Btm = [Buf("tm0"), Buf("tm1")]
                        Bpa = [Buf("pa0"), Buf("pa1")]
                        Bpb = [Buf("pb0"), Buf("pb1")]
                        gt_v = gt_d.rearrange("(two b p) t -> p two b t", p=128, two=2)
                        for db in range(32):
                            s = db % 2
                            q0, q1, q2 = (3 * db) % 4, (3 * db + 1) % 4, (3 * db + 2) % 4
                            dma("pool", "wq%d" % q0, wq[q0][:], wcb_d[db].rearrange("p (k c) -> p k c", c=128),
                                [], [Bwq[q0]])
                            dma("pool", "wq%d" % q1, wq[q1][:], wsb_d[2 * db].rearrange("p (k c) -> p k c", c=128),
                                [], [Bwq[q1]])
                            dma("pool", "wq%d" % q2, wq[q2][:], wsb_d[2 * db + 1].rearrange("p (k c) -> p k c", c=128),
                                [], [Bwq[q2]])
                            dma("sp", "p4gt%d" % s, gts[s][:], gt_v[:, :, db, t0:t0 + 512], [B_scr["gt_d"]], [Bgts[s]])
                            for cbk in range(32):
                                S.add("pe", lambda h, s=s, cbk=cbk, q0=q0: h.matmul(
                                    pa[s][:], lhsT=wq[q0][:, cbk, :], rhs=vv[:, cbk, :], start=(cbk == 0),
                                    stop=(cbk == 31)), [Bwq[q0], Bvv], [Bpa[s]])
                            for cbk in range(64):
                                qq = q1 if cbk < 32 else q2
                                S.add("pe", lambda h, s=s, cbk=cbk, qq=qq: h.matmul(
                                    pb[s][:], lhsT=wq[qq][:, cbk % 32, :], rhs=yz[:, cbk, :], start=(cbk == 0),
                                    stop=(cbk == 63)), [Bwq[qq], Byzg[cbk // 8]], [Bpb[s]])
                            S.add("dve", lambda h, s=s: h.tensor_tensor(out=tm[s][:], in0=pa[s][:], in1=gts[s][:, 0, :],
                                                                        op=ALU.mult), [Bpa[s], Bgts[s]], [Btm[s]])
                            S.add("dve", lambda h, s=s: h.tensor_tensor(out=gts[s][:, 1, :], in0=pb[s][:],
                                                                        in1=gts[s][:, 1, :], op=ALU.mult),
                                  [Bpb[s], Bgts[s]], [Bgts[s]])
                            S.add("dve", lambda h, s=s, db=db: h.tensor_tensor(
                                out=mrg[:, db, :], in0=tm[s][:], in1=gts[s][:, 1, :], op=ALU.add),
                                [Btm[s], Bgts[s]], [Bmrg])
                        S.flush()
                with contextlib.ExitStack() as st:
                    wo = [sb_t(st, "wo%d" % i, [128, 32, 256], BF16) for i in range(2)]
                    r = [sb_t(st, "r%d" % i, [128, D], F32) for i in range(4)]
                    g_b = sb_t(st, "og_b", [128, D], F32)
                    b_b = sb_t(st, "ob_b", [128, D], F32)
                    stt = sb_t(st, "ostt", [128, 8, 6], F32)
                    mv = sb_t(st, "omv", [128, 4], F32)
                    po = [ps_t(st, "p4o%d" % i, [128, 256], F32) for i in range(4)]
                    Bwo = [Buf("wo0"), Buf("wo1")]
                    Br = [Buf("r%d" % i) for i in range(4)]
                    Bgb, Bstt = Buf("ogb"), Buf("ostt")
                    Bpo = [Buf("po%d" % i) for i in range(4)]
                    dma("sp", "p4g", g_b[:], lnog, [], [Bgb])
                    dma("sp", "p4g", b_b[:], lnob, [], [Bgb])
                    for sub in range(4):
                        r0 = tok0 + t0 + sub * 128
                        dma("sp", "p4h%d" % sub, r[sub][:], h_d[r0:r0 + 128, :], [B_scr["h_d"]], [Br[sub]])
                    pc = 0
                    for e in range(16):
                        s = e % 2
                        dma("pool", "wo%d" % s, wo[s][:], wob_d[e].rearrange("p (k c) -> p k c", c=256),
                            [], [Bwo[s]])
                        for sub in range(4):
                            p_ = po[pc % 4]
                            bp = Bpo[pc % 4]
                            pc += 1
                            for db in range(32):
                                S.add("pe", lambda h, p_=p_, db=db, sub=sub, s=s: h.matmul(
                                    p_[:], lhsT=mrg[:, db, sub * 128:(sub + 1) * 128], rhs=wo[s][:, db, :],
                                    start=(db == 0), stop=(db == 31)), [Bmrg, Bwo[s]], [bp])
                            rs_ = r[sub][:, e * 256:(e + 1) * 256]
                            S.add("dve", lambda h, rs_=rs_, p_=p_: h.scalar_tensor_tensor(
                                out=rs_, in0=rs_, scalar=ALPHA, in1=p_[:], op0=ALU.mult, op1=ALU.add),
                                [Br[sub], bp], [Br[sub]])
                    for sub in range(4):
                        x_ = r[sub]
                        for c in range(8):
                            S.add("dve", lambda h, c=c, x_=x_: h.bn_stats(out=stt[:, c, :], in_=x_[:, c * 512:(c + 1) * 512]),
                                  [Br[sub]], [Bstt])
                        S.add("dve", lambda h: h.bn_aggr(out=mv[:, 0:2], in_=stt[:]), [Bstt], [Bstt])
                        S.add("act", lambda h: h.activation(out=mv[:, 2:3], in_=mv[:, 1:2], func=AF.Sqrt,
                                                            bias=eps_t[:, 0:1]), [Bstt, B_const], [Bstt])
                        S.add("dve", lambda h: h.reciprocal(out=mv[:, 2:3], in_=mv[:, 2:3]), [Bstt], [Bstt])
                        S.add("dve", lambda h, x_=x_: h.tensor_scalar(
                            out=x_[:], in0=x_[:], scalar1=mv[:, 0:1], scalar2=mv[:, 2:3], op0=ALU.subtract, op1=ALU.mult),
                            [Br[sub], Bstt], [Br[sub]])
                        S.add("dve", lambda h, x_=x_: h.tensor_tensor(out=x_[:], in0=x_[:], in1=g_b[:], op=ALU.mult),
                              [Br[sub], Bgb], [Br[sub]])
                        S.add("dve", lambda h, x_=x_: h.tensor_tensor(out=x_[:], in0=x_[:], in1=b_b[:], op=ALU.add),
                              [Br[sub], Bgb], [Br[sub]])
                        r0 = tok0 + t0 + sub * 128
                        dma("sp", "out", out[r0:r0 + 128, :], x_[:], [Br[sub]], [])
                    S.flush()
                stM.close()

        conv_jobs = []
        for db in range(32):
            conv_jobs.append((wcb_d[db].rearrange("p (k c) -> p k c", c=128), w_co_v[:, :, db * 128:(db + 1) * 128]))
            conv_jobs.append((wsb_d[2 * db].rearrange("p (k c) -> p k c", c=128),
                              w_so_v[:, 0:32, db * 128:(db + 1) * 128]))
            conv_jobs.append((wsb_d[2 * db + 1].rearrange("p (k c) -> p k c", c=128),
                              w_so_v[:, 32:64, db * 128:(db + 1) * 128]))
        for e in range(16):
            conv_jobs.append((wob_d[e].rearrange("p (k c) -> p k c", c=256), w_o_v[:, :, e * 256:(e + 1) * 256]))
        B_wconv = Buf("wconv")

        upto = debug.get("upto", 99) if debug else 99
        for si, (kind, tok0, T) in enumerate(STAGES):
            if si > upto:
                break
            last_pre = (kind == "pre" and STAGES[si + 1][0] == "main")
            with contextlib.ExitStack() as stH:
                hT = sb_t(stH, "hT", [128, KC, TMAX], BF16)
                phase1(kind, tok0, T)
                phase2(kind, tok0, T, last_pre)
            phase3(kind, tok0, T)
            if kind == "main":
                assert not conv_jobs, len(conv_jobs)
                phase4(tok0, T)
    return nc


def make_in_maps(inputs):
    f = lambda a: np.ascontiguousarray(np.asarray(a, dtype=np.float32))
    x = f(inputs["x"])
    meta = f(inputs["meta_tokens"])
    ident = np.eye(128, dtype=np.float32)
    tri = np.triu(np.ones((128, 128), np.float32))
    negm = np.where(tri > 0, 0.0, -30000.0).astype(np.float32)
    ones = np.ones((128, 128), np.float32)
    consts = np.concatenate([ident, tri, negm, ones], axis=1)

    def pmajor(v, nb):
        return np.ascontiguousarray(v.reshape(nb, 128).T)

    def rep(v):
        return np.ascontiguousarray(np.broadcast_to(v[None, :], (128, v.shape[0])))

    shared = {
        "w_in": f(inputs["w_in"][0]),
        "w_conv_out": f(inputs["w_conv_out"][0]),
        "w_ssd_out": f(inputs["w_ssd_out"][0]),
        "w_out": f(inputs["w_out"][0]),
        "ln_in_g_b": rep(f(inputs["ln_in_g"])),
        "ln_in_b_b": rep(f(inputs["ln_in_b"])),
        "ln_out_g_b": rep(f(inputs["ln_out_g"][0])),
        "ln_out_b_b": rep(f(inputs["ln_out_b"][0])),
        "consts": consts,
        "conv_w_p": np.ascontiguousarray(f(inputs["conv_w"][0]).T.reshape(32, 128, 31).transpose(1, 0, 2)),
        "conv_b_p": pmajor(f(inputs["conv_b"][0]), 32),
        "conv_ln_g_p": pmajor(f(inputs["conv_ln_g"][0]), 32),
        "conv_ln_b_p": pmajor(f(inputs["conv_ln_b"][0]), 32),
        "ssd_conv_w_p": np.ascontiguousarray(f(inputs["ssd_conv_w"][0]).T.reshape(80, 128, 4).transpose(1, 0, 2)),
        "ssd_conv_b_p": pmajor(f(inputs["ssd_conv_b"][0]), 80),
        "b_gate_p": pmajor(f(inputs["b_gate"][0]), 64),
        "ssd_norm_g_p": pmajor(f(inputs["ssd_norm_g"][0]), 64),
        "dt_bias_b": rep(f(inputs["dt_bias"][0])),
        "a_log_b": rep(f(inputs["a_log"][0])),
        "d_skip_b": rep(f(inputs["d_skip"][0])),
    }
    maps = []
    for c in range(8):
        b, half = c // 2, c % 2
        xpre = np.zeros((TPRE, D), np.float32)
        m = np.zeros((TPRE,), np.float32)
        if half == 0:
            xpre[TPRE - NMETA:] = meta
            m[TPRE - NMETA:] = 1.0
            xmain = x[b, 0:TMAIN]
        else:
            xpre[112:128] = meta
            xpre[128:] = x[b, 0:TMAIN]
            m[112:] = 1.0
            xmain = x[b, TMAIN:SEQ]
        d = dict(shared)
        d["xpre"] = xpre
        d["xmain"] = np.ascontiguousarray(xmain)
        d["mrow"] = np.ascontiguousarray(np.broadcast_to(m[None, :], (128, TPRE)))
        d["mtok"] = np.ascontiguousarray(m.reshape(TPRE // 128, 128).T)
        maps.append(d)
    return maps


def kernel(**inputs):
    nc = build_program()
    maps = make_in_maps(inputs)
    res = run_bass_kernel_spmd(nc, maps, core_ids=list(range(8)))
    outp = np.empty((4, SEQ, D), np.float32)
    for c in range(8):
        b, half = c // 2, c % 2
        outp[b, half * TMAIN:(half + 1) * TMAIN] = res.results[c]["out"]
    return outp
```

```python
import numpy as np
import concourse.bass as bass
import concourse.mybir as mybir
from concourse.bass_utils import run_bass_kernel_spmd

F32 = mybir.dt.float32
BF16 = mybir.dt.bfloat16
AF = mybir.ActivationFunctionType
ALU = mybir.AluOpType

D = 4096
KC = 32
NMETA = 16
SEQ = 4096
NH = 128
HD = 64
NG = 8
NST = 128
INNER = 8192
XBC = 10240
C_VAL, C_GLU, C_GATE = 0, 4096, 8192
C_Z = 12288
C_XBC = 20480
C_DT = 30720
C_GATES = 30848
IN_COLS = 39040
ALPHA = 2.0 ** 0.25
EPS = 1e-5
TMAIN = 2048
TPRE = 2176
STAGES = [("pre", 0, 1152), ("pre", 1152, 1024), ("main", 0, 1024), ("main", 1024, 1024)]
TMAX = 1152
SAME_ENGINE_SYNC = True


class Buf:
    __slots__ = ("name", "lw", "rd")

    def __init__(self, name):
        self.name = name
        self.lw = None
        self.rd = {}


class Op:
    __slots__ = ("eng", "fn", "deps", "marked", "semval", "dkey", "dcount", "idx")


class Sched:
    ENGS = ["pe", "act", "dve", "pool", "sp"]

    def __init__(self, nc, sems):
        self.nc = nc
        self.free_sems = list(sems)
        self.esem = {e: self.free_sems.pop() for e in ["pe", "act", "dve", "pool"]}
        self.ecnt = {e: 0 for e in self.ENGS}
        self.ops = {e: [] for e in self.ENGS}
        self.base = {e: 0 for e in self.ENGS}
        self.dsem = {}
        self.dcnt = {}
        self.waited = {e: {} for e in self.ENGS}
        self.nops = 0

    def _dma_sem(self, key):
        if key not in self.dsem:
            self.dsem[key] = self.free_sems.pop()
            self.dcnt[key] = 0
        return self.dsem[key]

    def add(self, eng, fn, reads=(), writes=(), dkey=None):
        deps = set()
        raw = set()
        for b in reads:
            if b.lw is not None:
                deps.add(b.lw)
                raw.add(b.lw)
        for b in writes:
            if b.lw is not None:
                deps.add(b.lw)
            for r in b.rd.values():
                deps.add(r)
        op = Op()
        op.eng = eng
        op.fn = fn
        op.marked = False
        op.semval = None
        op.dkey = dkey
        op.idx = self.base[eng] + len(self.ops[eng])
        if dkey is not None:
            self._dma_sem(dkey)
            self.dcnt[dkey] += 1
            op.dcount = self.dcnt[dkey]
            me = ("dma", dkey, op.dcount)
            grp = "dma:" + dkey
        else:
            op.dcount = None
            me = ("eng", eng, op.idx)
            grp = eng
        fd = []
        for d in deps:
            if d[0] == "eng" and d[1] == eng and dkey is None:
                if eng == "pe" or not SAME_ENGINE_SYNC or d not in raw:
                    continue
            fd.append(d)
        op.deps = fd
        self.ops[eng].append(op)
        for b in reads:
            b.rd[grp] = me
        for b in writes:
            b.lw = me
            b.rd = {}
        self.nops += 1
        return op

    def flush(self):
        nc = self.nc
        for e in self.ENGS:
            for op in self.ops[e]:
                nd = []
                for d in op.deps:
                    if d[0] == "eng":
                        i = d[2] - self.base[d[1]]
                        if i < 0:
                            continue
                        self.ops[d[1]][i].marked = True
                    else:
                        pass
                    nd.append(d)
                op.deps = nd
        for e in ["pe", "act", "dve", "pool"]:
            for op in reversed(self.ops[e]):
                if op.dkey is None:
                    op.marked = True
                    break
        for e in ["pe", "act", "dve", "pool"]:
            for op in self.ops[e]:
                if op.dkey is None and op.marked:
                    self.ecnt[e] += 1
                    op.semval = self.ecnt[e]
        final_e = dict(self.ecnt)
        final_d = {k: 16 * v for k, v in self.dcnt.items()}

        def emit(e, h):
            waited = self.waited[e]

            def wait(sem, val, key):
                if waited.get(key, 0) >= val:
                    return
                h.wait_ge(sem, val)
                waited[key] = val

            for op in self.ops[e]:
                for d in op.deps:
                    if d[0] == "eng":
                        src = self.ops[d[1]][d[2] - self.base[d[1]]]
                        wait(self.esem[d[1]], src.semval, d[1])
                    else:
                        wait(self.dsem[d[1]], 16 * d[2], "dma:" + d[1])
                ins = op.fn(h)
                if op.dkey is not None:
                    ins.then_inc(self.dsem[op.dkey], 16)
                elif op.marked:
                    ins.then_inc(self.esem[e], 1)
            for e2 in ["pe", "act", "dve", "pool"]:
                if final_e[e2] > 0:
                    wait(self.esem[e2], final_e[e2], e2)
            for k, v in final_d.items():
                if v > 0:
                    wait(self.dsem[k], v, "dma:" + k)

        with nc.Block() as block:
            @block.tensor
            def _(h):
                emit("pe", h)

            @block.scalar
            def _(h):
                emit("act", h)

            @block.vector
            def _(h):
                emit("dve", h)

            @block.gpsimd
            def _(h):
                emit("pool", h)

            @block.sync
            def _(h):
                emit("sp", h)

        for e in self.ENGS:
            self.base[e] += len(self.ops[e])
            self.ops[e] = []


def build_program(debug=False):
    nc = bass.Bass("TRN2", target_bir_lowering=False)

    def din(name, shape, dt=F32):
        return nc.dram_tensor(name, list(shape), dt, kind="ExternalInput").ap()

    xpre = din("xpre", [TPRE, D])
    xmain = din("xmain", [TMAIN, D])
    mrow = din("mrow", [128, TPRE])
    mtok = din("mtok", [128, TPRE // 128])
    w_in = din("w_in", [D, IN_COLS])
    w_co = din("w_conv_out", [D, D])
    w_so = din("w_ssd_out", [INNER, D])
    w_o = din("w_out", [D, D])
    lnig = din("ln_in_g_b", [128, D])
    lnib = din("ln_in_b_b", [128, D])
    lnog = din("ln_out_g_b", [128, D])
    lnob = din("ln_out_b_b", [128, D])
    consts = din("consts", [128, 512])
    cw = din("conv_w_p", [128, 32, 31])
    cb = din("conv_b_p", [128, 32])
    clg = din("conv_ln_g_p", [128, 32])
    clb = din("conv_ln_b_p", [128, 32])
    sw = din("ssd_conv_w_p", [128, 80, 4])
    sb = din("ssd_conv_b_p", [128, 80])
    bg = din("b_gate_p", [128, 64])
    gn = din("ssd_norm_g_p", [128, 64])
    dtb = din("dt_bias_b", [128, NH])
    alog = din("a_log_b", [128, NH])
    dsk = din("d_skip_b", [128, NH])
    out = nc.dram_tensor("out", [TMAIN, D], F32, kind="ExternalOutput").ap()

    def scratch(name, shape, dt):
        kind = "ExternalOutput" if (debug and name in debug) else "Internal"
        return nc.dram_tensor(name, list(shape), dt, kind=kind).ap()

    h_d = scratch("h_d", [TMAIN, D], F32)
    xbc_d = scratch("xbc_d", [XBC, TMAX], BF16)
    cvo_d = scratch("cvo_d", [D, TMAX], BF16)
    cg_d = scratch("cg_d", [D, TMAX], BF16)
    z_d = scratch("z_d", [INNER, TMAX], BF16)
    gt_d = scratch("gt_d", [2 * D, TMAX], BF16)
    yT_d = scratch("yT_d", [INNER, TMAX], BF16)
    S_d = scratch("S_d", [128, INNER], F32)
    wcb_d = scratch("wcb_d", [32, 128, 4096], BF16)
    wsb_d = scratch("wsb_d", [64, 128, 4096], BF16)
    wob_d = scratch("wob_d", [16, 128, 8192], BF16)
    dbg_dt = scratch("dbg_dt", [TMAX, NH], F32)

    w_in_v = w_in.rearrange("(kc p) c -> p kc c", p=128)
    w_co_v = w_co.rearrange("(kc p) c -> p kc c", p=128)
    w_so_v = w_so.rearrange("(kc p) c -> p kc c", p=128)
    w_o_v = w_o.rearrange("(kc p) c -> p kc c", p=128)

    import contextlib
    es = contextlib.ExitStack()
    with es:
        sems = [es.enter_context(nc.semaphore("s%d" % i)) for i in range(72)]
        S = Sched(nc, sems)

        uid = [0]

        def sb_t(st, name, shape, dt):
            uid[0] += 1
            return st.enter_context(nc.sbuf_tensor("%s_%d" % (name, uid[0]), list(shape), dt))

        def ps_t(st, name, shape, dt):
            uid[0] += 1
            return st.enter_context(nc.psum_tensor("%s_%d" % (name, uid[0]), list(shape), dt))

        cst = sb_t(es, "cst", [128, 512], F32)
        cstb = sb_t(es, "cstb", [128, 512], BF16)
        ident_f, tri_f, ones_f = cst[:, 0:128], cst[:, 128:256], cst[:, 384:512]
        ident_b, tri_b, negm_b, ones_b = (cstb[:, 0:128], cstb[:, 128:256],
                                          cstb[:, 256:384], cstb[:, 384:512])
        cw_t = sb_t(es, "cw_t", [128, 32, 31], F32)
        cb_t = sb_t(es, "cb_t", [128, 32], F32)
        clg_t = sb_t(es, "clg_t", [128, 32], F32)
        clb_t = sb_t(es, "clb_t", [128, 32], F32)
        sw_t = sb_t(es, "sw_t", [128, 80, 4], F32)
        sb_tl = sb_t(es, "sb_tl", [128, 80], F32)
        bg_t = sb_t(es, "bg_t", [128, 64], F32)
        gn_t = sb_t(es, "gn_t", [128, 64], F32)
        dtb_t = sb_t(es, "dtb_t", [128, NH], F32)
        a_t = sb_t(es, "a_t", [128, NH], F32)
        dsk_t = sb_t(es, "dsk_t", [128, NH], F32)
        mtok_t = sb_t(es, "mtok_t", [128, TPRE // 128], F32)
        eps_t = sb_t(es, "eps_t", [128, 2], F32)
        xh_t = sb_t(es, "xh_t", [128, 80, 3], F32)
        vh_t = sb_t(es, "vh_t", [128, 32, 30], F32)
        dt_sb = sb_t(es, "dt_sb", [128, TMAX // 128, NH], F32)
        hT = None
        B_const = Buf("const")
        B_hT = Buf("hT")
        B_dt = Buf("dt")
        B_xh = Buf("xh")
        B_vh = Buf("vh")
        B_Sd = Buf("S_d")
        B_scr = {n: Buf(n) for n in ["h_d", "xbc_d", "cvo_d", "cg_d", "z_d", "gt_d", "yT_d"]}

        def dma(eng, key, out_ap, in_ap, reads, writes):
            S.add(eng, lambda h, o=out_ap, i=in_ap: h.dma_start(out=o, in_=i), reads, writes, dkey=key)

        k = 0
        for t_, src in [(cst, consts), (cw_t, cw), (cb_t, cb), (clg_t, clg), (clb_t, clb),
                        (sw_t, sw), (sb_tl, sb), (bg_t, bg), (gn_t, gn), (dtb_t, dtb),
                        (a_t, alog), (dsk_t, dsk), (mtok_t, mtok)]:
            dma("sp", "setup", t_[:], src, [], [B_const])
            k += 1
        S.add("dve", lambda h: h.tensor_copy(out=cstb[:], in_=cst[:]), [B_const], [B_const])
        S.add("act", lambda h: h.activation(out=a_t[:], in_=a_t[:], func=AF.Exp), [B_const], [B_const])
        S.add("dve", lambda h: h.tensor_scalar(out=a_t[:], in0=a_t[:], scalar1=-1.0, scalar2=None,
                                               op0=ALU.mult), [B_const], [B_const])
        S.add("dve", lambda h: h.memset(xh_t[:], 0.0), [], [B_xh])
        S.add("dve", lambda h: h.memset(eps_t[:], EPS), [], [B_const])
        S.add("dve", lambda h: h.memset(vh_t[:], 0.0), [], [B_vh])
        with contextlib.ExitStack() as st0:
            s0 = sb_t(st0, "s0", [128, INNER], F32)
            b0 = Buf("s0")
            S.add("dve", lambda h: h.memset(s0[:], 0.0), [], [b0])
            dma("sp", "Sd", S_d, s0[:], [b0], [B_Sd])
            S.flush()

        def phase1(kind, tok0, T):
            src = xpre if kind == "pre" else xmain
            with contextlib.ExitStack() as st:
                g_b = sb_t(st, "g_b", [128, D], F32)
                b_b = sb_t(st, "b_b", [128, D], F32)
                xb = [sb_t(st, "xb%d" % i, [128, D], F32) for i in range(3)]
                hb = [sb_t(st, "hb%d" % i, [128, D], BF16) for i in range(2)]
                stt = [sb_t(st, "stt%d" % i, [128, 8, 6], F32) for i in range(2)]
                mv = [sb_t(st, "mv%d" % i, [128, 4], F32) for i in range(2)]
                pst = [ps_t(st, "pst%d" % i, [128, 1024], BF16) for i in range(4)]
                Bgb = Buf("gb")
                Bx = [Buf("xb0"), Buf("xb1"), Buf("xb2")]
                Bh = [Buf("hb0"), Buf("hb1")]
                Bs = [Buf("st0"), Buf("st1")]
                Bp = [Buf("pst%d" % i) for i in range(4)]
                dma("sp", "p1g", g_b[:], lnig, [], [Bgb])
                dma("sp", "p1g", b_b[:], lnib, [], [Bgb])
                def x_load(i):
                    s3 = i % 3
                    r0 = tok0 + i * 128
                    dma("sp", "p1x%d" % s3, xb[s3][:], src[r0:r0 + 128, :], [], [Bx[s3]])

                for k_ in range(min(3, T // 128)):
                    x_load(k_)
                BhTq = [Buf("hTq%d" % q) for q in range(4)]

                def do_tile(i):
                    s = i % 2
                    x_, h_, st_, mv_ = xb[i % 3], hb[s], stt[s], mv[s]
                    r0 = tok0 + i * 128
                    for c in range(8):
                        S.add("dve", lambda h, c=c, x_=x_, st_=st_: h.bn_stats(
                            out=st_[:, c, :], in_=x_[:, c * 512:(c + 1) * 512]), [Bx[i % 3]], [Bs[s]])
                    S.add("dve", lambda h, st_=st_, mv_=mv_: h.bn_aggr(out=mv_[:, 0:2], in_=st_[:]),
                          [Bs[s]], [Bs[s]])
                    S.add("act", lambda h, mv_=mv_: h.activation(
                        out=mv_[:, 2:3], in_=mv_[:, 1:2], func=AF.Sqrt, bias=eps_t[:, 0:1]), [Bs[s], B_const], [Bs[s]])
                    S.add("dve", lambda h, mv_=mv_: h.reciprocal(out=mv_[:, 2:3], in_=mv_[:, 2:3]), [Bs[s]], [Bs[s]])
                    S.add("dve", lambda h, mv_=mv_: h.scalar_tensor_tensor(
                        out=mv_[:, 3:4], in0=mv_[:, 0:1], scalar=-1.0, in1=mv_[:, 2:3],
                        op0=ALU.mult, op1=ALU.mult), [Bs[s]], [Bs[s]])
                    S.add("act", lambda h, x_=x_, mv_=mv_: h.activation(
                        out=x_[:], in_=x_[:], func=AF.Identity, bias=mv_[:, 3:4], scale=mv_[:, 2:3]),
                        [Bx[i % 3], Bs[s]], [Bx[i % 3]])
                    S.add("pool", lambda h, x_=x_: h.tensor_tensor(out=x_[:], in0=x_[:], in1=g_b[:],
                                                                   op=ALU.mult), [Bx[i % 3], Bgb], [Bx[i % 3]])
                    yield
                    S.add("dve", lambda h, x_=x_: h.tensor_tensor(out=x_[:], in0=x_[:], in1=b_b[:],
                                                                  op=ALU.add), [Bx[i % 3], Bgb], [Bx[i % 3]])
                    if kind == "main":
                        dma("sp", "p1h%d" % (i % 3), h_d[r0:r0 + 128, :], x_[:], [Bx[i % 3]], [B_scr["h_d"]])
                    S.add("act", lambda h, x_=x_, h_=h_: h.activation(out=h_[:], in_=x_[:], func=AF.Copy),
                          [Bx[i % 3]], [Bh[s]])
                    for q in range(4):
                        for kk in range(8):
                            kc = q * 8 + kk
                            S.add("pe", lambda h, q=q, kk=kk, kc=kc, h_=h_: h.transpose(
                                out=pst[q][:, kk * 128:(kk + 1) * 128], in_=h_[:, kc * 128:(kc + 1) * 128],
                                identity=ident_b), [Bh[s], B_const], [Bp[q]])
                        dst = hT[:, q * 8:(q + 1) * 8, i * 128:(i + 1) * 128]
                        srcp = pst[q][:].rearrange("p (a b) -> p a b", b=128)
                        if q % 2 == 0:
                            S.add("act", lambda h, dst=dst, srcp=srcp: h.activation(
                                out=dst, in_=srcp, func=AF.Copy), [Bp[q]], [BhTq[q]])
                        else:
                            S.add("dve", lambda h, dst=dst, srcp=srcp: h.tensor_copy(out=dst, in_=srcp),
                                  [Bp[q]], [BhTq[q]])
                    if i + 3 < T // 128:
                        x_load(i + 3)

                nt = T // 128
                g_cur = do_tile(0)
                next(g_cur)
                for i in range(nt):
                    g_nxt = None
                    if i + 1 < nt:
                        g_nxt = do_tile(i + 1)
                        next(g_nxt)
                    next(g_cur, None)
                    g_cur = g_nxt
                S.flush()

        def phase2(kind, tok0, T, last_pre):
            tts = []
            t = 0
            while t < T:
                n = min(512, T - t)
                tts.append((t, n))
                t += n
            NW = 3
            with contextlib.ExitStack() as st:
                wb = [sb_t(st, "wb%d" % i, [128, KC, 128], BF16) for i in range(NW)]
                Bw = [Buf("wb%d" % i) for i in range(NW)]
                pp = [ps_t(st, "pp%d" % i, [128, 512], F32) for i in range(8)]
                Bpp = [Buf("pp%d" % i) for i in range(8)]
                stg = sb_t(st, "stg", [128, 32 + TMAX], F32)
                acc = sb_t(st, "acc", [128, TMAX], F32)
                accs = [sb_t(st, "accs%d" % i, [128, TMAX], F32) for i in range(2)]
                vbs = [sb_t(st, "vbs%d" % i, [128, 32 + TMAX], BF16) for i in range(2)]
                dgs = [sb_t(st, "dgs%d" % i, [128, 15, 128], BF16) for i in range(2)]
                Baccs = [Buf("accs0"), Buf("accs1")]
                Bvbs = [Buf("vbs0"), Buf("vbs1")]
                Bdgs = [Buf("dgs0"), Buf("dgs1")]
                pending = []
                sig = sb_t(st, "sig", [128, TMAX], F32)
                ob = [sb_t(st, "ob%d" % i, [128, TMAX], BF16) for i in range(2)]
                mr = sb_t(st, "mr", [128, TMAX], F32)
                dtt = [sb_t(st, "dtt%d" % i, [128, NH], F32) for i in range(3)]
                Bstg, Bacc, Bsig, Bmr, Bdtt = Buf("stg"), Buf("acc"), Buf("sig"), Buf("mr"), Buf("dtt")
                Bob = [Buf("ob0"), Buf("ob1")]
                if kind == "pre":
                    dma("sp", "p2m", mr[:, 0:T], mrow[:, tok0:tok0 + T], [], [Bmr])
                blocks = []
                nx = 80 if (kind == "main" or last_pre) else 72
                for j in range(nx):
                    blocks.append(("xbc", j, C_XBC + j * 128))
                blocks.append(("dt", 0, C_DT))
                if kind == "main":
                    for j in range(64):
                        blocks.append(("z", j, C_Z + j * 128))
                if kind == "main" or last_pre:
                    for j in range(32):
                        blocks.append(("glu", j, C_GLU + j * 128))
                        blocks.append(("val", j, C_VAL + j * 128))
                if kind == "main":
                    for j in range(32):
                        blocks.append(("cg", j, C_GATE + j * 128))
                    for j in range(64):
                        blocks.append(("gt", j, C_GATES + j * 128))
                pcount = [0]
                ocount = [0]

                def next_ps():
                    i = pcount[0] % 8
                    pcount[0] += 1
                    return pp[i], Bpp[i]

                for bi, (bk, j, c0) in enumerate(blocks):
                    s = bi % NW
                    if pending and bk != "glu":
                        pending.pop(0)()
                    if bi >= NW and bi % 3 == 0 and conv_jobs:
                        o_ap, i_ap = conv_jobs.pop(0)
                        dma("pool", "wconv", o_ap, i_ap, [], [])
                    w_ = wb[s]
                    dma("pool", "w%d" % s, w_[:], w_in_v[:, :, c0:c0 + 128], [], [Bw[s]])
                    if bk == "dt":
                        for i in range(T // 128):
                            ps, bps = next_ps()
                            for kc in range(KC):
                                S.add("pe", lambda h, ps=ps, kc=kc, i=i, w_=w_: h.matmul(
                                    ps[:, 0:128], lhsT=hT[:, kc, i * 128:(i + 1) * 128], rhs=w_[:, kc, :],
                                    start=(kc == 0), stop=(kc == KC - 1)), [B_hT, Bw[s]], [bps])
                            t0_, t1_, t2_ = dtt
                            S.add("dve", lambda h, ps=ps: h.tensor_tensor(
                                out=t0_[:], in0=ps[:, 0:128], in1=dtb_t[:], op=ALU.add), [bps, B_const], [Bdtt])
                            S.add("act", lambda h: h.activation(out=t1_[:], in_=t0_[:], func=AF.Abs), [Bdtt], [Bdtt])
                            S.add("act", lambda h: h.activation(out=t1_[:], in_=t1_[:], func=AF.Exp, scale=-1.0),
                                  [Bdtt], [Bdtt])
                            S.add("act", lambda h: h.activation(out=t1_[:], in_=t1_[:], func=AF.Ln, bias=1.0),
                                  [Bdtt], [Bdtt])
                            dst = dt_sb[:, i, :]
                            S.add("dve", lambda h, dst=dst: h.scalar_tensor_tensor(
                                out=dst, in0=t0_[:], scalar=0.0, in1=t1_[:], op0=ALU.max, op1=ALU.add),
                                [Bdtt], [B_dt])
                            if kind == "pre":
                                ci = tok0 // 128 + i
                                S.add("dve", lambda h, dst=dst, ci=ci: h.tensor_scalar(
                                    out=dst, in0=dst, scalar1=mtok_t[:, ci:ci + 1], scalar2=None, op0=ALU.mult),
                                    [B_dt, B_const], [B_dt])
                        if debug and "dbg_dt" in debug:
                            for i in range(T // 128):
                                dma("sp", "dbg", dbg_dt[i * 128:(i + 1) * 128, :], dt_sb[:, i, :], [B_dt], [])
                        continue
                    halo_only = (kind == "pre" and bk in ("glu", "val"))
                    my_tts = tts[-1:] if halo_only else tts
                    for (t0, n) in my_tts:
                        ps, bps = next_ps()
                        for kc in range(KC):
                            S.add("pe", lambda h, ps=ps, kc=kc, t0=t0, n=n, w_=w_: h.matmul(
                                ps[:, 0:n], lhsT=w_[:, kc, :], rhs=hT[:, kc, t0:t0 + n],
                                start=(kc == 0), stop=(kc == KC - 1)), [B_hT, Bw[s]], [bps])
                        if bk == "xbc":
                            if kind == "pre":
                                S.add("dve", lambda h, ps=ps, t0=t0, n=n: h.tensor_tensor(
                                    out=stg[:, 3 + t0:3 + t0 + n], in0=ps[:, 0:n], in1=mr[:, t0:t0 + n],
                                    op=ALU.mult), [bps, Bmr], [Bstg])
                            else:
                                S.add("act", lambda h, ps=ps, t0=t0, n=n: h.activation(
                                    out=stg[:, 3 + t0:3 + t0 + n], in_=ps[:, 0:n], func=AF.Copy), [bps], [Bstg])
                        elif bk in ("z", "cg", "gt"):
                            o_ = ob[ocount[0] % 2]
                            bo = Bob[ocount[0] % 2]
                            if bk == "gt":
                                S.add("act", lambda h, ps=ps, t0=t0, n=n, o_=o_, j=j: h.activation(
                                    out=o_[:, t0:t0 + n], in_=ps[:, 0:n], func=AF.Sigmoid, bias=bg_t[:, j:j + 1]),
                                    [bps, B_const], [bo])
                            else:
                                S.add("act", lambda h, ps=ps, t0=t0, n=n, o_=o_: h.activation(
                                    out=o_[:, t0:t0 + n], in_=ps[:, 0:n], func=AF.Silu), [bps], [bo])
                        elif bk == "glu":
                            S.add("act", lambda h, ps=ps, t0=t0, n=n: h.activation(
                                out=sig[:, t0:t0 + n], in_=ps[:, 0:n], func=AF.Sigmoid), [bps], [Bsig])
                        elif bk == "val":
                            S.add("dve", lambda h, ps=ps, t0=t0, n=n: h.tensor_tensor(
                                out=stg[:, 30 + t0:30 + t0 + n], in0=ps[:, 0:n], in1=sig[:, t0:t0 + n],
                                op=ALU.mult), [bps, Bsig], [Bstg])
                            if kind == "pre":
                                S.add("dve", lambda h, t0=t0, n=n: h.tensor_tensor(
                                    out=stg[:, 30 + t0:30 + t0 + n], in0=stg[:, 30 + t0:30 + t0 + n],
                                    in1=mr[:, t0:t0 + n], op=ALU.mult), [Bstg, Bmr], [Bstg])
                    if bk == "xbc":
                        o_ = ob[ocount[0] % 2]
                        bo = Bob[ocount[0] % 2]
                        ocount[0] += 1
                        S.add("dve", lambda h, j=j: h.tensor_copy(out=stg[:, 0:3], in_=xh_t[:, j, :]),
                              [B_xh], [Bstg])
                        S.add("dve", lambda h, j=j: h.tensor_scalar(
                            out=acc[:, 0:T], in0=stg[:, 0:T], scalar1=sw_t[:, j, 0:1], scalar2=None,
                            op0=ALU.mult), [Bstg, B_const], [Bacc])
                        for kk in range(1, 4):
                            S.add("dve", lambda h, j=j, kk=kk: h.scalar_tensor_tensor(
                                out=acc[:, 0:T], in0=stg[:, kk:kk + T], scalar=sw_t[:, j, kk:kk + 1],
                                in1=acc[:, 0:T], op0=ALU.mult, op1=ALU.add), [Bstg, Bacc, B_const], [Bacc])
                        S.add("act", lambda h, j=j, o_=o_: h.activation(
                            out=o_[:, 0:T], in_=acc[:, 0:T], func=AF.Silu, bias=sb_tl[:, j:j + 1]),
                            [Bacc, B_const], [bo])
                        S.add("dve", lambda h, j=j: h.tensor_copy(out=xh_t[:, j, :], in_=stg[:, T:T + 3]),
                              [Bstg], [B_xh])
                        dma("sp", "p2o%d" % ((ocount[0] - 1) % 2), xbc_d[j * 128:(j + 1) * 128, 0:T], o_[:, 0:T],
                            [bo], [B_scr["xbc_d"]])
                    elif bk in ("z", "cg", "gt"):
                        o_ = ob[ocount[0] % 2]
                        bo = Bob[ocount[0] % 2]
                        ocount[0] += 1
                        dd = {"z": z_d, "cg": cg_d, "gt": gt_d}[bk]
                        dn = {"z": "z_d", "cg": "cg_d", "gt": "gt_d"}[bk]
                        dma("sp", "p2o%d" % ((ocount[0] - 1) % 2), dd[j * 128:(j + 1) * 128, 0:T], o_[:, 0:T],
                            [bo], [B_scr[dn]])
                    elif bk == "val":
                        if kind == "pre":
                            S.add("dve", lambda h, j=j: h.tensor_copy(out=vh_t[:, j, :], in_=stg[:, T:T + 30]),
                                  [Bstg], [B_vh])
                        else:
                            o_ = ob[ocount[0] % 2]
                            bo = Bob[ocount[0] % 2]
                            ocount[0] += 1
                            S.add("dve", lambda h, j=j: h.tensor_copy(out=stg[:, 0:30], in_=vh_t[:, j, :]),
                                  [B_vh], [Bstg])
                            NDV = 16
                            pj = j % 2
                            ac_, bac = accs[pj], Baccs[pj]
                            vb_, bvb = vbs[pj], Bvbs[pj]
                            dg_, bdg = dgs[pj], Bdgs[pj]
                            S.add("act", lambda h, vb_=vb_: h.activation(out=vb_[:, 0:30 + T], in_=stg[:, 0:30 + T],
                                                                         func=AF.Copy), [Bstg], [bvb])
                            for ii, kk in enumerate(range(NDV, 31)):
                                S.add("act", lambda h, j=j, kk=kk, ii=ii, dg_=dg_: h.activation(
                                    out=dg_[:, ii, :], in_=ident_b, func=AF.Copy, scale=cw_t[:, j, kk:kk + 1]),
                                    [B_const], [bdg])
                            S.add("dve", lambda h, j=j, ac_=ac_: h.tensor_scalar(
                                out=ac_[:, 0:T], in0=stg[:, 0:T], scalar1=cw_t[:, j, 0:1],
                                scalar2=cb_t[:, j:j + 1], op0=ALU.mult, op1=ALU.add), [Bstg, B_const], [bac])
                            for kk in range(1, NDV):
                                S.add("dve", lambda h, j=j, kk=kk, ac_=ac_: h.scalar_tensor_tensor(
                                    out=ac_[:, 0:T], in0=stg[:, kk:kk + T], scalar=cw_t[:, j, kk:kk + 1],
                                    in1=ac_[:, 0:T], op0=ALU.mult, op1=ALU.add), [Bstg, bac, B_const], [bac])
                            S.add("dve", lambda h, j=j: h.tensor_copy(out=vh_t[:, j, :], in_=stg[:, T:T + 30]),
                                  [Bstg], [B_vh])
                            oslot = (ocount[0] - 1) % 2

                            def pend(j=j, ac_=ac_, bac=bac, vb_=vb_, bvb=bvb, dg_=dg_, bdg=bdg, o_=o_, bo=bo, oslot=oslot):
                                for (t0, n) in tts:
                                    ps, bps = next_ps()
                                    for ii, kk in enumerate(range(NDV, 31)):
                                        S.add("pe", lambda h, ps=ps, ii=ii, kk=kk, t0=t0, n=n: h.matmul(
                                            ps[:, 0:n], lhsT=dg_[:, ii, :], rhs=vb_[:, kk + t0:kk + t0 + n],
                                            start=(ii == 0), stop=(kk == 30)), [bdg, bvb], [bps])
                                    S.add("dve", lambda h, ps=ps, t0=t0, n=n: h.tensor_tensor(
                                        out=o_[:, t0:t0 + n], in0=ac_[:, t0:t0 + n], in1=ps[:, 0:n], op=ALU.add),
                                        [bac, bps], [bo])
                                dma("sp", "p2o%d" % oslot, cvo_d[j * 128:(j + 1) * 128, 0:T],
                                    o_[:, 0:T], [bo], [B_scr["cvo_d"]])

                            pending.append(pend)
                while pending:
                    pending.pop(0)()
                S.flush()

        def phase3(kind, tok0, T):
            main = kind == "main"
            nblk = 80 if main else 72
            xbc_v = xbc_d.rearrange("(b p) t -> p b t", p=128)
            yT_v = yT_d.rearrange("(b p) t -> p b t", p=128)
            with contextlib.ExitStack() as st:
                Sf = sb_t(st, "Sf", [128, INNER], F32)
                Sb = sb_t(st, "Sb", [128, INNER] if main else [128, 2], BF16)
                xc = [sb_t(st, "xc%d" % i, [128, 80, 128], BF16) for i in range(2)]
                nb = 1 if main else 2
                xtoks = [sb_t(st, "xtok%d" % k_, [128, INNER + 1024], BF16) for k_ in range(nb)]
                xws = [sb_t(st, "xw%d" % k_, [128, INNER], BF16) for k_ in range(nb)]
                sms = [sb_t(st, "sm%d" % k_, [128, 8, 128], F32) for k_ in range(nb)]
                sbf = sb_t(st, "sbf", [128, 3, 128], BF16)
                cbm = sb_t(st, "cbm", [128, 8, 128], F32)
                Eh = [sb_t(st, "Eh%d" % i, [128, 512], F32) for i in range(2)]
                Mh = [sb_t(st, "Mh%d" % i, [128, 128], BF16) for i in range(8)]
                t1 = [sb_t(st, "t1_%d" % i, [128, 512], F32) for i in range(2)]
                t3 = [sb_t(st, "t3_%d" % i, [128, 512], F32) for i in range(2)]
                ytok = sb_t(st, "ytok", [128, INNER] if main else [128, 2], BF16)
                yTs = sb_t(st, "yTs", [128, 64, 128] if main else [128, 1, 2], BF16)
                psT = [ps_t(st, "psT%d" % i, [128, 1024], BF16) for i in range(2)]
                psA = [ps_t(st, "psA%d" % i, [128, 512], F32) for i in range(6)]
                BSf = [Buf("Sf%d" % i) for i in range(16)]
                BSb = [Buf("Sb%d" % i) for i in range(16)]
                Bxc = [Buf("xc0"), Buf("xc1")]
                Bsbf, Bcbm = Buf("sbf"), Buf("cbm")
                Bxtoks = [Buf("xtok%d" % k_) for k_ in range(nb)]
                Bxws = [Buf("xw%d" % k_) for k_ in range(nb)]
                Bsms = [Buf("sm%d" % k_) for k_ in range(nb)]
                BEh = [Buf("Eh%d" % i) for i in range(8)]
                BMh = [Buf("Mh%d" % i) for i in range(8)]
                Bt1 = [Buf("t10"), Buf("t11")]
                Bt3 = [Buf("t30"), Buf("t31")]
                Bytok, ByTs = Buf("ytok"), Buf("yTs")
                BpsT = [Buf("psT0"), Buf("psT1")]
                BpsA = [Buf("psA%d" % i) for i in range(6)]
                hi, lo, lo2 = [sbf[:, i, :] for i in range(3)]
                pac = [0]
                ptc = [0]

                def nA():
                    i = pac[0] % 6
                    pac[0] += 1
                    return psA[i], BpsA[i]

                def nT():
                    i = ptc[0] % 2
                    ptc[0] += 1
                    return psT[i], BpsT[i]

                dma("sp", "Sld", Sf[:], S_d, [B_Sd], BSf)
                for sl in range(16 if main else 0):
                    S.add("act", lambda h, sl=sl: h.activation(
                        out=Sb[:, sl * 512:(sl + 1) * 512], in_=Sf[:, sl * 512:(sl + 1) * 512], func=AF.Copy),
                        [BSf[sl]], [BSb[sl]])

                def bc8(ap2d):
                    return ap2d.unsqueeze(2).to_broadcast([128, 8, 64])

                nch = T // 128

                def xc_load(i):
                    cs = i % 2
                    dma("sp", "xc%d" % cs, xc[cs][:, 0:nblk, :], xbc_v[:, 0:nblk, i * 128:(i + 1) * 128],
                        [B_scr["xbc_d"]], [Bxc[cs]])

                xc_load(0)

                def do_chunk(i, xtok, xw, sm, Bxtok, Bxw, Bsm):
                    da, acs, nacs, wst, El, cd, tmp, r_ = [sm[:, k_, :] for k_ in range(8)]
                    cs = i % 2
                    xc_ = xc[cs]
                    if i + 1 < nch:
                        xc_load(i + 1)
                    for q in range(9):
                        pT, bT = nT()
                        for kk in range(8):
                            S.add("pe", lambda h, pT=pT, kk=kk, q=q, xc_=xc_: h.transpose(
                                out=pT[:, kk * 128:(kk + 1) * 128], in_=xc_[:, q * 8 + kk, :], identity=ident_b),
                                [Bxc[cs], B_const], [bT])
                        if True:
                            S.add("act", lambda h, pT=pT, q=q: h.activation(
                                out=xtok[:, q * 1024:(q + 1) * 1024], in_=pT[:], func=AF.Copy), [bT], [Bxtok])
                        else:
                            S.add("dve", lambda h, pT=pT, q=q: h.tensor_copy(
                                out=xtok[:, q * 1024:(q + 1) * 1024], in_=pT[:]), [bT], [Bxtok])
                    dti = dt_sb[:, i, :]
                    S.add("dve", lambda h, dti=dti: h.tensor_tensor(out=da, in0=dti, in1=a_t[:], op=ALU.mult),
                          [B_dt, B_const], [Bsm])
                    pS, bS = nA()
                    S.add("pe", lambda h, pS=pS: h.matmul(pS[:, 0:128], lhsT=tri_f, rhs=da, start=True, stop=True),
                          [Bsm, B_const], [bS])
                    S.add("pe", lambda h, pS=pS: h.matmul(pS[:, 128:256], lhsT=ones_f, rhs=da, start=True, stop=True),
                          [Bsm, B_const], [bS])
                    if main:
                        S.add("pe", lambda h, pS=pS: h.matmul(pS[:, 256:384], lhsT=da, rhs=tri_f, start=True,
                                                               stop=True), [Bsm, B_const], [bS])
                    S.add("act", lambda h, pS=pS: h.activation(out=cd, in_=pS[:, 128:256], func=AF.Exp), [bS], [Bsm])
                    S.add("dve", lambda h, pS=pS: h.tensor_copy(out=acs, in_=pS[:, 0:128]), [bS], [Bsm])
                    S.add("dve", lambda h, pS=pS: h.tensor_tensor(out=tmp, in0=pS[:, 128:256], in1=acs,
                                                                  op=ALU.subtract), [bS, Bsm], [Bsm])
                    S.add("act", lambda h: h.activation(out=wst, in_=tmp, func=AF.Exp), [Bsm], [Bsm])
                    S.add("dve", lambda h, dti=dti: h.tensor_tensor(out=wst, in0=wst, in1=dti, op=ALU.mult),
                          [Bsm, B_dt], [Bsm])
                    if main:
                        S.add("act", lambda h: h.activation(out=El, in_=acs, func=AF.Exp), [Bsm], [Bsm])
                        S.add("act", lambda h, dti=dti: h.activation(out=nacs, in_=dti, func=AF.Ln), [B_dt], [Bsm])
                        S.add("dve", lambda h: h.tensor_tensor(out=nacs, in0=nacs, in1=acs, op=ALU.subtract),
                              [Bsm], [Bsm])
                        S.add("act", lambda h, pS=pS: h.activation(out=hi, in_=pS[:, 256:384], func=AF.Copy),
                              [bS], [Bsbf])
                        S.add("dve", lambda h, pS=pS: h.tensor_tensor(out=r_, in0=pS[:, 256:384], in1=hi,
                                                                      op=ALU.subtract), [bS, Bsbf], [Bsm])
                        S.add("act", lambda h: h.activation(out=lo, in_=r_, func=AF.Copy), [Bsm], [Bsbf])
                    for hh in range(2):
                        S.add("pool" if main else "dve", lambda h, hh=hh: h.tensor_tensor(
                            out=xw[:, hh * 4096:(hh + 1) * 4096].rearrange("p (a b) -> p a b", b=64),
                            in0=xtok[:, hh * 4096:(hh + 1) * 4096].rearrange("p (a b) -> p a b", b=64),
                            in1=wst[:, hh * 64:(hh + 1) * 64].unsqueeze(2).to_broadcast([128, 64, 64]),
                            op=ALU.mult), [Bxtok, Bsm], [Bxw])
                    yield
                    if main:
                        for half in range(2):
                            pC, bC = nA()
                            for gg in range(4):
                                g = half * 4 + gg
                                S.add("pe", lambda h, pC=pC, gg=gg, g=g, xc_=xc_: h.matmul(
                                    pC[:, gg * 128:(gg + 1) * 128], lhsT=xc_[:, 64 + g, :], rhs=xc_[:, 72 + g, :],
                                    start=True, stop=True), [Bxc[cs]], [bC])
                            S.add("dve", lambda h, pC=pC, half=half: h.tensor_tensor(
                                out=cbm[:, half * 4:(half + 1) * 4, :],
                                in0=pC[:].rearrange("p (a b) -> p a b", b=128),
                                in1=tri_f.unsqueeze(1).to_broadcast([128, 4, 128]), op=ALU.mult),
                                [bC, B_const], [Bcbm])
                        pYs = {}

                        def rec_D(gq, xc_=xc_, dti=dti):
                            pD, bD = nA()
                            sl2 = gq % 2
                            e_ = Eh[sl2]
                            for hh in range(4):
                                hd = gq * 4 + hh
                                sel = ident_b[:, hd:hd + 1].to_broadcast([128, 128])
                                dsl = pD[:, hh * 128:(hh + 1) * 128]
                                S.add("pe", lambda h, dsl=dsl, sel=sel: h.matmul(dsl, lhsT=sel, rhs=hi, start=True,
                                                                                  stop=False), [Bsbf, B_const], [bD])
                                S.add("pe", lambda h, dsl=dsl, sel=sel: h.matmul(dsl, lhsT=sel, rhs=lo, start=False,
                                                                                  stop=False), [Bsbf, B_const], [bD])
                                S.add("pe", lambda h, dsl=dsl: h.matmul(dsl, lhsT=ident_b, rhs=negm_b, start=False,
                                                                        stop=True), [B_const], [bD])
                            for hh in range(4):
                                hd = gq * 4 + hh
                                S.add("act", lambda h, pD=pD, hh=hh, hd=hd, e_=e_: h.activation(
                                    out=e_[:, hh * 128:(hh + 1) * 128], in_=pD[:, hh * 128:(hh + 1) * 128],
                                    func=AF.Exp, bias=nacs[:, hd:hd + 1]), [bD, Bsm], [BEh[sl2 * 4 + hh]])
                            for hh in range(4):
                                hd = gq * 4 + hh
                                g = hd // 16
                                m_ = Mh[sl2 * 4 + hh]
                                S.add("dve" if hh % 2 == 0 else "pool",
                                      lambda h, e_=e_, hh=hh, g=g, m_=m_: h.tensor_tensor(
                                          out=m_[:], in0=e_[:, hh * 128:(hh + 1) * 128], in1=cbm[:, g, :], op=ALU.mult),
                                      [BEh[sl2 * 4 + hh], Bcbm], [BMh[sl2 * 4 + hh]])

                        def rec_Y(gq, xc_=xc_, cs=cs):
                            h8 = gq // 2
                            hq = gq % 2
                            sl2 = gq % 2
                            if hq == 0:
                                pYs[h8] = nA()
                            pY, bY = pYs[h8]
                            for hh in range(4):
                                hd = gq * 4 + hh
                                m_ = Mh[sl2 * 4 + hh]
                                yo = (hq * 4 + hh) * 64
                                S.add("pe", lambda h, pY=pY, yo=yo, m_=m_, hd=hd: h.matmul(
                                    pY[:, yo:yo + 64], lhsT=m_[:], rhs=xtok[:, hd * 64:(hd + 1) * 64],
                                    start=True, stop=True), [BMh[sl2 * 4 + hh], Bxtok], [bY])
                            if hq == 0:
                                return
                            g = h8 // 2
                            pO, bO = nA()
                            S.add("pe", lambda h, pO=pO, g=g, h8=h8: h.matmul(
                                pO[:], lhsT=xc_[:, 72 + g, :], rhs=Sb[:, h8 * 512:(h8 + 1) * 512],
                                start=True, stop=True), [Bxc[cs], BSb[h8]], [bO])
                            ta, tb = t1[h8 % 2], t3[h8 % 2]
                            Bta, Btb = Bt1[h8 % 2], Bt3[h8 % 2]
                            S.add("dve", lambda h, pO=pO, ta=ta, h8=h8: h.tensor_tensor(
                                out=ta[:].rearrange("p (a b) -> p a b", b=64),
                                in0=pO[:].rearrange("p (a b) -> p a b", b=64),
                                in1=bc8(El[:, h8 * 8:(h8 + 1) * 8]), op=ALU.mult), [bO, Bsm], [Bta])
                            S.add("dve", lambda h, pY=pY, ta=ta: h.tensor_tensor(
                                out=ta[:], in0=ta[:], in1=pY[:], op=ALU.add), [bY, Bta], [Bta])
                            S.add("pool", lambda h, tb=tb, h8=h8: h.tensor_tensor(
                                out=tb[:].rearrange("p (a b) -> p a b", b=64),
                                in0=xtok[:, h8 * 512:(h8 + 1) * 512].rearrange("p (a b) -> p a b", b=64),
                                in1=bc8(dsk_t[:, h8 * 8:(h8 + 1) * 8]), op=ALU.mult), [Bxtok, B_const], [Btb])
                            S.add("pool", lambda h, ta=ta, tb=tb, h8=h8: h.tensor_tensor(
                                out=ytok[:, h8 * 512:(h8 + 1) * 512], in0=ta[:], in1=tb[:], op=ALU.add),
                                [Bta, Btb], [Bytok])

                        rec_D(0)
                        for gq in range(1, 32):
                            rec_D(gq)
                            rec_Y(gq - 1)
                        rec_Y(31)
                        for q in range(8):
                            pT, bT = nT()
                            for kk in range(8):
                                S.add("pe", lambda h, pT=pT, kk=kk, q=q: h.transpose(
                                    out=pT[:, kk * 128:(kk + 1) * 128],
                                    in_=ytok[:, (q * 8 + kk) * 128:(q * 8 + kk + 1) * 128], identity=ident_b),
                                    [Bytok, B_const], [bT])
                            dst = yTs[:, q * 8:(q + 1) * 8, :]
                            srcp = pT[:].rearrange("p (a b) -> p a b", b=128)
                            if True:
                                S.add("act", lambda h, dst=dst, srcp=srcp: h.activation(out=dst, in_=srcp, func=AF.Copy),
                                      [bT], [ByTs])
                            else:
                                S.add("dve", lambda h, dst=dst, srcp=srcp: h.tensor_copy(out=dst, in_=srcp),
                                      [bT], [ByTs])
                        dma("sp", "yTo", yT_v[:, :, i * 128:(i + 1) * 128], yTs[:], [ByTs], [B_scr["yT_d"]])
                    for sl in range(16):
                        g = sl // 2
                        pQ, bQ = nA()
                        S.add("pe", lambda h, pQ=pQ, g=g, sl=sl: h.matmul(
                            pQ[:], lhsT=xtok[:, INNER + g * 128:INNER + (g + 1) * 128],
                            rhs=xw[:, sl * 512:(sl + 1) * 512], start=True, stop=True), [Bxtok, Bxw], [bQ])
                        ssl = Sf[:, sl * 512:(sl + 1) * 512]
                        S.add("pool", lambda h, ssl=ssl, sl=sl: h.tensor_tensor(
                            out=ssl.rearrange("p (a b) -> p a b", b=64), in0=ssl.rearrange("p (a b) -> p a b", b=64),
                            in1=bc8(cd[:, sl * 8:(sl + 1) * 8]), op=ALU.mult), [BSf[sl], Bsm], [BSf[sl]])
                        S.add("dve", lambda h, ssl=ssl, pQ=pQ: h.tensor_tensor(out=ssl, in0=ssl, in1=pQ[:], op=ALU.add),
                              [BSf[sl], bQ], [BSf[sl]])
                        if main and i < nch - 1:
                            S.add("act", lambda h, ssl=ssl, sl=sl: h.activation(
                                out=Sb[:, sl * 512:(sl + 1) * 512], in_=ssl, func=AF.Copy), [BSf[sl]], [BSb[sl]])
                def start(i):
                    k_ = i % nb
                    g_ = do_chunk(i, xtoks[k_], xws[k_], sms[k_], Bxtoks[k_], Bxws[k_], Bsms[k_])
                    next(g_)
                    return g_

                if main:
                    for i in range(nch):
                        g_ = start(i)
                        next(g_, None)
                else:
                    g_cur = start(0)
                    for i in range(nch):
                        g_nxt = start(i + 1) if i + 1 < nch else None
                        next(g_cur, None)
                        g_cur = g_nxt
                dma("sp", "Sst", S_d, Sf[:], BSf, [B_Sd])
                S.flush()

        def phase4(tok0, T):
            cvo_v = cvo_d.rearrange("(b p) t -> p b t", p=128)
            cg_v = cg_d.rearrange("(b p) t -> p b t", p=128)
            yT_v = yT_d.rearrange("(b p) t -> p b t", p=128)
            for tt in range(T // 512):
                t0 = tt * 512
                stM = contextlib.ExitStack()
                mrg = sb_t(stM, "mrg", [128, 32, 512], BF16)
                Bmrg = Buf("mrg")
                with contextlib.ExitStack() as stA:
                    vv = sb_t(stA, "vv", [128, 32, 512], BF16)
                    yz = sb_t(stA, "yz", [128, 64, 512], BF16)
                    Bvv = Buf("vv")
                    Byzg = [Buf("yz%d" % k_) for k_ in range(8)]
                    with contextlib.ExitStack() as st:
                        cv = sb_t(st, "cv", [128, 32, 512], BF16)
                        cgs = [sb_t(st, "cgs%d" % i, [128, 512], BF16) for i in range(4)]
                        Bcgs = [Buf("cgs%d" % i) for i in range(4)]
                        sq = [sb_t(st, "sq%d" % i, [128, 512], BF16) for i in range(2)]
                        rs = sb_t(st, "rs", [128, 4, 512], F32)
                        u = [sb_t(st, "u%d" % i, [128, 512], F32) for i in range(2)]
                        p1 = ps_t(st, "p4s1", [128, 512], F32)
                        p2 = ps_t(st, "p4s2", [128, 512], F32)
                        Bcv, Brs = Buf("cv"), Buf("rs")
                        Bsq = [Buf("sq0"), Buf("sq1")]
                        Bu = [Buf("u0"), Buf("u1")]
                        Bp1, Bp2 = Buf("p1"), Buf("p2")
                        Bcvq = [Buf("cv%d" % k_) for k_ in range(4)]
                        for k_ in range(4):
                            dma("sp", "p4cv%d" % k_, cv[:, k_ * 8:(k_ + 1) * 8, :], cvo_v[:, k_ * 8:(k_ + 1) * 8, t0:t0 + 512],
                                [B_scr["cvo_d"]], [Bcvq[k_]])
                        def cg_load(b):
                            dma("sp", "p4cg%d" % (b % 4), cgs[b % 4][:], cg_d[b * 128:(b + 1) * 128, t0:t0 + 512],
                                [B_scr["cg_d"]], [Bcgs[b % 4]])

                        for b in range(4):
                            cg_load(b)
                        dma("sp", "p4y", yz[:], yT_v[:, :, t0:t0 + 512], [B_scr["yT_d"]], Byzg)
                        for b in range(32):
                            S.add("act", lambda h, b=b: h.activation(out=sq[b % 2][:], in_=cv[:, b, :], func=AF.Square),
                                  [Bcvq[b // 8]], [Bsq[b % 2]])
                            S.add("pe", lambda h, b=b: h.matmul(p1[:], lhsT=ones_b, rhs=cv[:, b, :], start=(b == 0),
                                                                stop=(b == 31)), [Bcvq[b // 8], B_const], [Bp1])
                            S.add("pe", lambda h, b=b: h.matmul(p2[:], lhsT=ones_b, rhs=sq[b % 2][:], start=(b == 0),
                                                                stop=(b == 31)), [Bsq[b % 2], B_const], [Bp2])
                        mean, var, rstd, nmr = [rs[:, i, :] for i in range(4)]
                        S.add("dve", lambda h: h.tensor_scalar(out=mean, in0=p1[:], scalar1=1.0 / D, scalar2=None,
                                                               op0=ALU.mult), [Bp1], [Brs])
                        S.add("dve", lambda h: h.tensor_tensor(out=var, in0=mean, in1=mean, op=ALU.mult), [Brs], [Brs])
                        S.add("dve", lambda h: h.scalar_tensor_tensor(out=var, in0=p2[:], scalar=1.0 / D, in1=var,
                                                                      op0=ALU.mult, op1=ALU.subtract), [Bp2, Brs], [Brs])
                        S.add("act", lambda h: h.activation(out=rstd, in_=var, func=AF.Sqrt, bias=eps_t[:, 0:1]),
                              [Brs, B_const], [Brs])
                        S.add("dve", lambda h: h.reciprocal(out=rstd, in_=rstd), [Brs], [Brs])
                        S.add("dve", lambda h: h.scalar_tensor_tensor(out=nmr, in0=mean, scalar=-1.0, in1=rstd,
                                                                      op0=ALU.mult, op1=ALU.mult), [Brs], [Brs])
                        for b in range(32):
                            u_ = u[b % 2]
                            S.add("dve", lambda h, b=b, u_=u_: h.tensor_tensor(out=u_[:], in0=cv[:, b, :], in1=rstd,
                                                                               op=ALU.mult), [Bcvq[b // 8], Brs], [Bu[b % 2]])
                            S.add("dve", lambda h, u_=u_: h.tensor_tensor(out=u_[:], in0=u_[:], in1=nmr, op=ALU.add),
                                  [Bu[b % 2], Brs], [Bu[b % 2]])
                            S.add("act", lambda h, b=b, u_=u_: h.activation(
                                out=u_[:], in_=u_[:], func=AF.Silu, bias=clb_t[:, b:b + 1], scale=clg_t[:, b:b + 1]),
                                [Bu[b % 2], B_const], [Bu[b % 2]])
                            S.add("dve", lambda h, b=b, u_=u_: h.tensor_tensor(out=vv[:, b, :], in0=u_[:], in1=cgs[b % 4][:],
                                                                                op=ALU.mult), [Bu[b % 2], Bcgs[b % 4]], [Bvv])
                            if b + 4 < 32:
                                cg_load(b + 4)
                        S.flush()
                    with contextlib.ExitStack() as st:
                        zt = [sb_t(st, "zt%d" % i, [128, 8, 512], BF16) for i in range(2)]
                        sq = [sb_t(st, "sq%d" % i, [128, 512], BF16) for i in range(2)]
                        rg = sb_t(st, "rg", [128, 8, 512], F32)
                        pg = [ps_t(st, "p4g%d" % i, [128, 512], F32) for i in range(2)]
                        Bzt = [Buf("zt0"), Buf("zt1")]
                        Bsq = [Buf("sq0"), Buf("sq1")]
                        Brg = Buf("rg")
                        Bpg = [Buf("pg0"), Buf("pg1")]
                        z_v = z_d.rearrange("(b p) t -> p b t", p=128)
                        Brgg = [Buf("rg%d" % k_) for k_ in range(8)]

                        def zmul(g):
                            zs = g % 2
                            dma("sp", "p4z%d" % zs, zt[zs][:], z_v[:, g * 8:(g + 1) * 8, t0:t0 + 512],
                                [B_scr["z_d"]], [Bzt[zs]])
                            S.add("dve", lambda h, g=g, zs=zs: h.tensor_tensor(
                                out=yz[:, g * 8:(g + 1) * 8, :], in0=yz[:, g * 8:(g + 1) * 8, :], in1=zt[zs][:],
                                op=ALU.mult), [Byzg[g], Bzt[zs]], [Byzg[g]])

                        zmul(0)
                        for g in range(8):
                            if g + 1 < 8:
                                zmul(g + 1)
                            for bb in range(8):
                                b = g * 8 + bb
                                S.add("act", lambda h, b=b: h.activation(out=sq[b % 2][:], in_=yz[:, b, :],
                                                                         func=AF.Square), [Byzg[g]], [Bsq[b % 2]])
                                S.add("pe", lambda h, b=b, bb=bb, g=g: h.matmul(
                                    pg[g % 2][:], lhsT=ones_b, rhs=sq[b % 2][:], start=(bb == 0), stop=(bb == 7)),
                                    [Bsq[b % 2], B_const], [Bpg[g % 2]])
                            S.add("act", lambda h, g=g: h.activation(
                                out=rg[:, g, :], in_=pg[g % 2][:], func=AF.Sqrt, bias=eps_t[:, 0:1], scale=1.0 / 1024),
                                [Bpg[g % 2], B_const], [Brgg[g]])
                            S.add("dve", lambda h, g=g: h.reciprocal(out=rg[:, g, :], in_=rg[:, g, :]), [Brgg[g]], [Brgg[g]])
                            for bb in range(8):
                                b = g * 8 + bb
                                S.add("dve", lambda h, b=b, g=g: h.scalar_tensor_tensor(
                                    out=yz[:, b, :], in0=yz[:, b, :], scalar=gn_t[:, b:b + 1], in1=rg[:, g, :],
                                    op0=ALU.mult, op1=ALU.mult), [Byzg[g], Brgg[g], B_const], [Byzg[g]])
                        S.flush()
                    with contextlib.ExitStack() as st:
                        wq = [sb_t(st, "wq%d" % i, [128, 32, 128], BF16) for i in range(4)]
                        Bwq = [Buf("wq%d" % i) for i in range(4)]
                        gts = [sb_t(st, "gts%d" % i, [128, 2, 512], BF16) for i in range(2)]
                        tm = [sb_t(st, "tm%d" % i, [128, 512], F32) for i in range(2)]
                        pa = [ps_t(st, "p4a%d" % i, [128, 512], F32) for i in range(2)]
                        pb = [ps_t(st, "p4b%d" % i, [128, 512], F32) for i in range(2)]
                        Bgts = [Buf("g0"), Buf("g1")]
                        Btm = [Buf("tm0"), Buf("tm1")]
                        Bpa = [Buf("pa0"), Buf("pa1")]
                        Bpb = [Buf("pb0"), Buf("pb1")]
                        gt_v = gt_d.rearrange("(two b p) t -> p two b t", p=128, two=2)
                        for db in range(32):
                            s = db % 2
                            q0, q1, q2 = (3 * db) % 4, (3 * db + 1) % 4, (3 * db + 2) % 4
                            dma("pool", "wq%d" % q0, wq[q0][:], wcb_d[db].rearrange("p (k c) -> p k c", c=128),
                                [], [Bwq[q0]])
                            dma("pool", "wq%d" % q1, wq[q1][:], wsb_d[2 * db].rearrange("p (k c) -> p k c", c=128),
                                [], [Bwq[q1]])
                            dma("pool", "wq%d" % q2, wq[q2][:], wsb_d[2 * db + 1].rearrange("p (k c) -> p k c", c=128),
                                [], [Bwq[q2]])
                            dma("sp", "p4gt%d" % s, gts[s][:], gt_v[:, :, db, t0:t0 + 512], [B_scr["gt_d"]], [Bgts[s]])
                            for cbk in range(32):
                                S.add("pe", lambda h, s=s, cbk=cbk, q0=q0: h.matmul(
                                    pa[s][:], lhsT=wq[q0][:, cbk, :], rhs=vv[:, cbk, :], start=(cbk == 0),
                                    stop=(cbk == 31)), [Bwq[q0], Bvv], [Bpa[s]])
                            for cbk in range(64):
                                qq = q1 if cbk < 32 else q2
                                S.add("pe", lambda h, s=s, cbk=cbk, qq=qq: h.matmul(
                                    pb[s][:], lhsT=wq[qq][:, cbk % 32, :], rhs=yz[:, cbk, :], start=(cbk == 0),
                                    stop=(cbk == 63)), [Bwq[qq], Byzg[cbk // 8]], [Bpb[s]])
                            S.add("dve", lambda h, s=s: h.tensor_tensor(out=tm[s][:], in0=pa[s][:], in1=gts[s][:, 0, :],
                                                                        op=ALU.mult), [Bpa[s], Bgts[s]], [Btm[s]])
                            S.add("dve", lambda h, s=s: h.tensor_tensor(out=gts[s][:, 1, :], in0=pb[s][:],
                                                                        in1=gts[s][:, 1, :], op=ALU.mult),
                                  [Bpb[s], Bgts[s]], [Bgts[s]])
                            S.add("dve", lambda h, s=s, db=db: h.tensor_tensor(
                                out=mrg[:, db, :], in0=tm[s][:], in1=gts[s][:, 1, :], op=ALU.add),
                                [Btm[s], Bgts[s]], [Bmrg])
                        S.flush()
                with contextlib.ExitStack() as st:
                    wo = [sb_t(st, "wo%d" % i, [128, 32, 256], BF16) for i in range(2)]
                    r = [sb_t(st, "r%d" % i, [128, D], F32) for i in range(4)]
                    g_b = sb_t(st, "og_b", [128, D], F32)
                    b_b = sb_t(st, "ob_b", [128, D], F32)
                    stt = sb_t(st, "ostt", [128, 8, 6], F32)
                    mv = sb_t(st, "omv", [128, 4], F32)
                    po = [ps_t(st, "p4o%d" % i, [128, 256], F32) for i in range(4)]
                    Bwo = [Buf("wo0"), Buf("wo1")]
                    Br = [Buf("r%d" % i) for i in range(4)]
                    Bgb, Bstt = Buf("ogb"), Buf("ostt")
                    Bpo = [Buf("po%d" % i) for i in range(4)]
                    dma("sp", "p4g", g_b[:], lnog, [], [Bgb])
                    dma("sp", "p4g", b_b[:], lnob, [], [Bgb])
                    for sub in range(4):
                        r0 = tok0 + t0 + sub * 128
                        dma("sp", "p4h%d" % sub, r[sub][:], h_d[r0:r0 + 128, :], [B_scr["h_d"]], [Br[sub]])
                    pc = 0
                    for e in range(16):
                        s = e % 2
                        dma("pool", "wo%d" % s, wo[s][:], wob_d[e].rearrange("p (k c) -> p k c", c=256),
                            [], [Bwo[s]])
                        for sub in range(4):
                            p_ = po[pc % 4]
                            bp = Bpo[pc % 4]
                            pc += 1
                            for db in range(32):
                                S.add("pe", lambda h, p_=p_, db=db, sub=sub, s=s: h.matmul(
                                    p_[:], lhsT=mrg[:, db, sub * 128:(sub + 1) * 128], rhs=wo[s][:, db, :],
                                    start=(db == 0), stop=(db == 31)), [Bmrg, Bwo[s]], [bp])
                            rs_ = r[sub][:, e * 256:(e + 1) * 256]
                            S.add("dve", lambda h, rs_=rs_, p_=p_: h.scalar_tensor_tensor(
                                out=rs_, in0=rs_, scalar=ALPHA, in1=p_[:], op0=ALU.mult, op1=ALU.add),
                                [Br[sub], bp], [Br[sub]])
                    for sub in range(4):
                        x_ = r[sub]
                        for c in range(8):
                            S.add("dve", lambda h, c=c, x_=x_: h.bn_stats(out=stt[:, c, :], in_=x_[:, c * 512:(c + 1) * 512]),
                                  [Br[sub]], [Bstt])
                        S.add("dve", lambda h: h.bn_aggr(out=mv[:, 0:2], in_=stt[:]), [Bstt], [Bstt])
                        S.add("act", lambda h: h.activation(out=mv[:, 2:3], in_=mv[:, 1:2], func=AF.Sqrt,
                                                            bias=eps_t[:, 0:1]), [Bstt, B_const], [Bstt])
                        S.add("dve", lambda h: h.reciprocal(out=mv[:, 2:3], in_=mv[:, 2:3]), [Bstt], [Bstt])
                        S.add("dve", lambda h, x_=x_: h.tensor_scalar(
                            out=x_[:], in0=x_[:], scalar1=mv[:, 0:1], scalar2=mv[:, 2:3], op0=ALU.subtract, op1=ALU.mult),
                            [Br[sub], Bstt], [Br[sub]])
                        S.add("dve", lambda h, x_=x_: h.tensor_tensor(out=x_[:], in0=x_[:], in1=g_b[:], op=ALU.mult),
                              [Br[sub], Bgb], [Br[sub]])
                        S.add("dve", lambda h, x_=x_: h.tensor_tensor(out=x_[:], in0=x_[:], in1=b_b[:], op=ALU.add),
                              [Br[sub], Bgb], [Br[sub]])
                        r0 = tok0 + t0 + sub * 128
                        dma("sp", "out", out[r0:r0 + 128, :], x_[:], [Br[sub]], [])
                    S.flush()
                stM.close()

        conv_jobs = []
        for db in range(32):
            conv_jobs.append((wcb_d[db].rearrange("p (k c) -> p k c", c=128), w_co_v[:, :, db * 128:(db + 1) * 128]))
            conv_jobs.append((wsb_d[2 * db].rearrange("p (k c) -> p k c", c=128),
                              w_so_v[:, 0:32, db * 128:(db + 1) * 128]))
            conv_jobs.append((wsb_d[2 * db + 1].rearrange("p (k c) -> p k c", c=128),
                              w_so_v[:, 32:64, db * 128:(db + 1) * 128]))
        for e in range(16):
            conv_jobs.append((wob_d[e].rearrange("p (k c) -> p k c", c=256), w_o_v[:, :, e * 256:(e + 1) * 256]))
        B_wconv = Buf("wconv")

        upto = debug.get("upto", 99) if debug else 99
        for si, (kind, tok0, T) in enumerate(STAGES):
            if si > upto:
                break
            last_pre = (kind == "pre" and STAGES[si + 1][0] == "main")
            with contextlib.ExitStack() as stH:
                hT = sb_t(stH, "hT", [128, KC, TMAX], BF16)
                phase1(kind, tok0, T)
                phase2(kind, tok0, T, last_pre)
            phase3(kind, tok0, T)
            if kind == "main":
                assert not conv_jobs, len(conv_jobs)
                phase4(tok0, T)
    return nc


def make_in_maps(inputs):
    f = lambda a: np.ascontiguousarray(np.asarray(a, dtype=np.float32))
    x = f(inputs["x"])
    meta = f(inputs["meta_tokens"])
    ident = np.eye(128, dtype=np.float32)
    tri = np.triu(np.ones((128, 128), np.float32))
    negm = np.where(tri > 0, 0.0, -30000.0).astype(np.float32)
    ones = np.ones((128, 128), np.float32)
    consts = np.concatenate([ident, tri, negm, ones], axis=1)

    def pmajor(v, nb):
        return np.ascontiguousarray(v.reshape(nb, 128).T)

    def rep(v):
        return np.ascontiguousarray(np.broadcast_to(v[None, :], (128, v.shape[0])))

    shared = {
        "w_in": f(inputs["w_in"][0]),
        "w_conv_out": f(inputs["w_conv_out"][0]),
        "w_ssd_out": f(inputs["w_ssd_out"][0]),
        "w_out": f(inputs["w_out"][0]),
        "ln_in_g_b": rep(f(inputs["ln_in_g"])),
        "ln_in_b_b": rep(f(inputs["ln_in_b"])),
        "ln_out_g_b": rep(f(inputs["ln_out_g"][0])),
        "ln_out_b_b": rep(f(inputs["ln_out_b"][0])),
        "consts": consts,
        "conv_w_p": np.ascontiguousarray(f(inputs["conv_w"][0]).T.reshape(32, 128, 31).transpose(1, 0, 2)),
        "conv_b_p": pmajor(f(inputs["conv_b"][0]), 32),
        "conv_ln_g_p": pmajor(f(inputs["conv_ln_g"][0]), 32),
        "conv_ln_b_p": pmajor(f(inputs["conv_ln_b"][0]), 32),
        "ssd_conv_w_p": np.ascontiguousarray(f(inputs["ssd_conv_w"][0]).T.reshape(80, 128, 4).transpose(1, 0, 2)),
        "ssd_conv_b_p": pmajor(f(inputs["ssd_conv_b"][0]), 80),
        "b_gate_p": pmajor(f(inputs["b_gate"][0]), 64),
        "ssd_norm_g_p": pmajor(f(inputs["ssd_norm_g"][0]), 64),
        "dt_bias_b": rep(f(inputs["dt_bias"][0])),
        "a_log_b": rep(f(inputs["a_log"][0])),
        "d_skip_b": rep(f(inputs["d_skip"][0])),
    }
    maps = []
    for c in range(8):
        b, half = c // 2, c % 2
        xpre = np.zeros((TPRE, D), np.float32)
        m = np.zeros((TPRE,), np.float32)
        if half == 0:
            xpre[TPRE - NMETA:] = meta
            m[TPRE - NMETA:] = 1.0
            xmain = x[b, 0:TMAIN]
        else:
            xpre[112:128] = meta
            xpre[128:] = x[b, 0:TMAIN]
            m[112:] = 1.0
            xmain = x[b, TMAIN:SEQ]
        d = dict(shared)
        d["xpre"] = xpre
        d["xmain"] = np.ascontiguousarray(xmain)
        d["mrow"] = np.ascontiguousarray(np.broadcast_to(m[None, :], (128, TPRE)))
        d["mtok"] = np.ascontiguousarray(m.reshape(TPRE // 128, 128).T)
        maps.append(d)
    return maps


def kernel(**inputs):
    nc = build_program()
    maps = make_in_maps(inputs)
    res = run_bass_kernel_spmd(nc, maps, core_ids=list(range(8)))
    outp = np.empty((4, SEQ, D), np.float32)
    for c in range(8):
        b, half = c // 2, c % 2
        outp[b, half * TMAIN:(half + 1) * TMAIN] = res.results[c]["out"]
    return outp
```
